# Optimizing a Trainium2 kernel written in Bass

```python
import math
import jax, jax.numpy as jnp
from jax import lax
import numpy as np

D_MODEL = 1024
BATCH = 16
SEQ = 4096
DEPTH = 1

CHUNK = 64
D_MIX = D_MODEL
POOL_WIDTH = D_MIX // 2
POOL_WINDOWS = (2, 4, 8, 16)
POOL_GROUPS = len(POOL_WINDOWS)
POOL_GW = POOL_WIDTH // POOL_GROUPS
HGRN_WIDTH = D_MIX - POOL_WIDTH
HGRN_HEADS = 4
HGRN_DK = HGRN_WIDTH // HGRN_HEADS
HGRN_DV = HGRN_WIDTH // HGRN_HEADS
IN_COLS = POOL_WIDTH + 2 * HGRN_HEADS * HGRN_DK + 2 * HGRN_HEADS * HGRN_DV
N_MEM = 256
XATTN_HEADS = 4
XATTN_HD = D_MODEL // XATTN_HEADS
D_FF = 4 * D_MODEL
EPS = 1e-6

kernel_name = "hybrid_pool_hgrn2_xattn_block"


def rmsnorm(x, gain):
    xf = x.astype(jnp.float32)
    y = xf * lax.rsqrt(jnp.mean(xf * xf, axis=-1, keepdims=True) + EPS)
    return (y * gain.astype(jnp.float32)).astype(x.dtype)


def pool_mixer(u, w_grp, scale):
    B, S, P = u.shape
    uf = u.astype(jnp.float32)
    cs = jnp.concatenate([jnp.zeros((B, 1, P), jnp.float32), jnp.cumsum(uf, axis=1)], axis=1)
    t = jnp.arange(S)
    outs = []
    for g, w in enumerate(POOL_WINDOWS):
        sl = slice(g * POOL_GW, (g + 1) * POOL_GW)
        lo = jnp.maximum(t + 1 - w, 0)
        win_sum = cs[:, 1:, sl] - cs[:, lo, sl]
        cnt = jnp.minimum(t + 1, w).astype(jnp.float32)[None, :, None]
        outs.append(win_sum / cnt - uf[..., sl])
    y = jnp.stack(outs, axis=2).astype(u.dtype)
    y = jnp.einsum('bsgc,gcd->bsgd', y, w_grp).reshape(B, S, P)
    return y * scale


def hgrn2_chunkwise(q, log_f, k, v):
    B, S, H, DK = q.shape
    DV = v.shape[-1]
    n = S // CHUNK

    def to_chunks(a):
        return a.reshape(B, n, CHUNK, H, a.shape[-1]).transpose(0, 1, 3, 2, 4)

    q, log_f, k, v = map(to_chunks, (q, log_f, k, v))
    G = jnp.cumsum(log_f, axis=3)
    G_last = G[:, :, :, -1:]
    G_mid = G[:, :, :, CHUNK // 2 - 1:CHUNK // 2]

    q_rel = q * jnp.exp(G - G_mid)
    k_rel = k * jnp.exp(G_mid - G)
    scores = jnp.einsum('bnhtk,bnhsk->bnhts', q_rel, k_rel)
    causal = jnp.tril(jnp.ones((CHUNK, CHUNK), dtype=bool))
    scores = jnp.where(causal, scores, jnp.zeros_like(scores))
    o_intra = jnp.einsum('bnhts,bnhsv->bnhtv', scores, v)

    k_end = k * jnp.exp(G_last - G)
    dS = jnp.einsum('bnhsk,bnhsv->bnhkv', k_end, v)
    decay = jnp.exp(G_last[:, :, :, 0, :])

    def step(state, inp):
        dS_c, d_c = inp
        return d_c[..., None] * state + dS_c, state

    S0 = jnp.zeros((B, H, DK, DV), q.dtype)
    _, S_prev = lax.scan(step, S0, (dS.transpose(1, 0, 2, 3, 4), decay.transpose(1, 0, 2, 3)))
    S_prev = S_prev.transpose(1, 0, 2, 3, 4)
    o_inter = jnp.einsum('bnhtk,bnhkv->bnhtv', q * jnp.exp(G), S_prev)
    o = o_intra + o_inter
    return o.transpose(0, 1, 3, 2, 4).reshape(B, S, H, DV)


def hgrn2_mixer(z, lb_theta, layer, o_norm):
    B, S, _ = z.shape
    hk = HGRN_HEADS * HGRN_DK
    hv = HGRN_HEADS * HGRN_DV
    zq, zf, zi, zg = jnp.split(z, [hk, 2 * hk, 2 * hk + hv], axis=-1)
    p = jax.nn.softmax(lb_theta.astype(jnp.float32), axis=0)
    lb = jnp.cumsum(p, axis=0)[layer]
    f = lb + (1.0 - lb) * jax.nn.sigmoid(zf.astype(jnp.float32))
    log_f = jnp.log(f).astype(z.dtype)
    k = (1.0 - f).astype(z.dtype)
    q = jax.nn.silu(zq)
    shp = (B, S, HGRN_HEADS, -1)
    o = hgrn2_chunkwise(q.reshape(shp), log_f.reshape(shp), k.reshape(shp), zi.reshape(shp))
    o = rmsnorm(o, o_norm.reshape(HGRN_HEADS, HGRN_DV))
    return o.reshape(B, S, hv) * jax.nn.silu(zg)


def cross_attention(h, mem, wq, wkv, wo):
    B, S, _ = h.shape
    q = (h @ wq).reshape(B, S, XATTN_HEADS, XATTN_HD)
    kv = mem @ wkv
    k, v = jnp.split(kv, 2, axis=-1)
    k = k.reshape(B, N_MEM, XATTN_HEADS, XATTN_HD)
    v = v.reshape(B, N_MEM, XATTN_HEADS, XATTN_HD)
    s = jnp.einsum('bshd,bmhd->bhsm', q, k).astype(jnp.float32) / math.sqrt(XATTN_HD)
    p = jax.nn.softmax(s, axis=-1).astype(h.dtype)
    o = jnp.einsum('bhsm,bmhd->bshd', p, v).reshape(B, S, D_MODEL)
    return o @ wo


def setup_inputs(seed: int = 0) -> dict:
    key = jax.random.key(seed)
    ks = jax.random.split(key, 24)
    n = jax.random.normal
    L = DEPTH
    return {
        "x": n(ks[0], (BATCH, SEQ, D_MODEL), jnp.float32),
        "mem": n(ks[1], (BATCH, N_MEM, D_MODEL), jnp.float32),
        "norm_mix": 1.0 + 0.02 * n(ks[2], (L, D_MODEL), jnp.float32),
        "w_in": n(ks[3], (L, D_MODEL, IN_COLS), jnp.float32) * D_MODEL ** -0.5,
        "pool_w": n(ks[4], (L, POOL_GROUPS, POOL_GW, POOL_GW), jnp.float32) * POOL_GW ** -0.5,
        "pool_scale": 1.0 + 0.02 * n(ks[5], (L, POOL_WIDTH), jnp.float32),
        "lb_theta": 0.1 * n(ks[6], (L + 1, HGRN_HEADS * HGRN_DK), jnp.float32),
        "hgrn_norm": 1.0 + 0.02 * n(ks[7], (L, HGRN_HEADS * HGRN_DV), jnp.float32),
        "w_out": n(ks[8], (L, D_MIX, D_MODEL), jnp.float32) * D_MIX ** -0.5,
        "norm_xq": 1.0 + 0.02 * n(ks[9], (L, D_MODEL), jnp.float32),
        "norm_mem": 1.0 + 0.02 * n(ks[10], (L, D_MODEL), jnp.float32),
        "xw_q": n(ks[11], (L, D_MODEL, D_MODEL), jnp.float32) * D_MODEL ** -0.5,
        "xw_kv": n(ks[12], (L, D_MODEL, 2 * D_MODEL), jnp.float32) * D_MODEL ** -0.5,
        "xw_o": n(ks[13], (L, D_MODEL, D_MODEL), jnp.float32) * D_MODEL ** -0.5,
        "norm_mlp": 1.0 + 0.02 * n(ks[14], (L, D_MODEL), jnp.float32),
        "w_up": n(ks[15], (L, D_MODEL, D_FF), jnp.float32) * D_MODEL ** -0.5,
        "w_down": n(ks[16], (L, D_FF, D_MODEL), jnp.float32) * D_FF ** -0.5,
        "norm_final": 1.0 + 0.02 * n(ks[17], (D_MODEL,), jnp.float32),
    }


def reference(x, mem, norm_mix, w_in, pool_w, pool_scale, lb_theta, hgrn_norm, w_out,
              norm_xq, norm_mem, xw_q, xw_kv, xw_o, norm_mlp, w_up, w_down, norm_final):
    h = x
    for l in range(DEPTH):
        u = rmsnorm(h, norm_mix[l]) @ w_in[l]
        u_pool, u_hgrn = u[..., :POOL_WIDTH], u[..., POOL_WIDTH:]
        y_pool = pool_mixer(u_pool, pool_w[l], pool_scale[l])
        y_hgrn = hgrn2_mixer(u_hgrn, lb_theta, l, hgrn_norm[l])
        h = h + jnp.concatenate([y_pool, y_hgrn], axis=-1) @ w_out[l]
        h = h + cross_attention(rmsnorm(h, norm_xq[l]), rmsnorm(mem, norm_mem[l]),
                                xw_q[l], xw_kv[l], xw_o[l])
        a = jax.nn.relu(rmsnorm(h, norm_mlp[l]) @ w_up[l])
        h = h + (a * a) @ w_down[l]
    return rmsnorm(h, norm_final)
```

```python
import contextlib

import numpy as np
import concourse.bass as bass
import concourse.mybir as mybir
from concourse.bass_utils import run_bass_kernel_spmd

dt = mybir.dt
F32 = dt.float32
BF16 = dt.bfloat16
AF = mybir.ActivationFunctionType
ALU = mybir.AluOpType

NCORES = 8
D = 1024
SEQ = 4096
T = 512
NSEQ = 2
NMEM = 256
EPS = 1e-6

U_WINF = 0
U_WINT = 12
U_POOLW = 20
U_WOUT = 21
U_XQ = 29
U_XO = 37
U_WUP = 45
U_WDN = 77
U_XK = 109
U_XV = 117
NU = 125


def _fm(w, c):
    blk = w[:, c * 128:(c + 1) * 128]
    return blk.reshape(8, 128, 128).transpose(1, 0, 2).reshape(128, 1024)


def _host_layout(inp):
    w_in = np.asarray(inp["w_in"], np.float32)[0]
    wall = np.zeros((NU, 128, 1024), np.float32)
    for hh in range(4):
        wall[U_WINF + hh] = _fm(w_in, 8 + hh)
        wall[U_WINF + 4 + hh] = _fm(w_in, 4 + hh)
        wall[U_WINF + 8 + hh] = _fm(w_in, 16 + hh)
    for kc in range(8):
        wall[U_WINT + kc, :, 0:512] = w_in[kc * 128:(kc + 1) * 128, 0:512]
        wall[U_WINT + kc, :, 512:1024] = w_in[kc * 128:(kc + 1) * 128, 1536:2048]
    pw = np.asarray(inp["pool_w"], np.float32)[0]
    for g in range(4):
        wall[U_POOLW, :, g * 128:(g + 1) * 128] = pw[g]
    w_out = np.asarray(inp["w_out"], np.float32)[0]
    xw_q = np.asarray(inp["xw_q"], np.float32)[0]
    xw_o = np.asarray(inp["xw_o"], np.float32)[0]
    xw_kv = np.asarray(inp["xw_kv"], np.float32)[0]
    w_up = np.asarray(inp["w_up"], np.float32)[0]
    w_dn = np.asarray(inp["w_down"], np.float32)[0]
    for kc in range(8):
        wall[U_WOUT + kc] = w_out[kc * 128:(kc + 1) * 128]
        wall[U_XO + kc] = xw_o[kc * 128:(kc + 1) * 128]
        wall[U_XQ + kc] = _fm(xw_q, kc)
        wall[U_XK + kc] = _fm(xw_kv, kc)
        wall[U_XV + kc] = xw_kv[kc * 128:(kc + 1) * 128, 1024:2048]
    for j in range(32):
        wall[U_WUP + j] = _fm(w_up, j)
        wall[U_WDN + j] = w_dn[j * 128:(j + 1) * 128]

    def pk(v):
        return np.asarray(v, np.float32).reshape(8, 128).T

    gains = np.zeros((128, 5, 8), np.float32)
    gains[:, 0] = pk(inp["norm_mix"][0])
    gains[:, 1] = pk(inp["norm_xq"][0])
    gains[:, 2] = pk(inp["norm_mem"][0])
    gains[:, 3] = pk(inp["norm_mlp"][0])
    gains[:, 4] = pk(np.concatenate([np.asarray(inp["pool_scale"], np.float32)[0],
                                     np.asarray(inp["hgrn_norm"], np.float32)[0]]))
    th = np.asarray(inp["lb_theta"], np.float32)
    theta = np.ascontiguousarray(th.reshape(2, 4, 128).transpose(2, 0, 1))
    gfin = np.ascontiguousarray(np.broadcast_to(np.asarray(inp["norm_final"], np.float32), (128, 1024)))
    return wall, np.ascontiguousarray(gains), theta, gfin


def _host_consts():
    ident = np.eye(128, dtype=np.float32)
    s = np.arange(128)[:, None]
    t = np.arange(128)[None, :]
    cmask = ((s // 64 == t // 64) & (s <= t)).astype(np.float32)
    pband = np.zeros((128, 12, 128), np.float32)
    for g, w in enumerate((2, 4, 8, 16)):
        inwin = ((t - s) >= 0) & ((t - s) < w)
        pband[:, g, :] = inwin / w - (s == t)
        d = t + 128 - s
        pband[:, 4 + g, :] = ((d >= 0) & (d < w)) / w
        cnt = np.minimum(t + 1, w)
        pband[:, 8 + g, :] = inwin / cnt - (s == t)
    return ident, cmask, pband


class Buf:
    __slots__ = ("name", "w", "r", "excl")

    def __init__(self, name="", excl=False):
        self.name = name
        self.w = None
        self.r = []
        self.excl = excl


class Op:
    __slots__ = ("eng", "fn", "deps", "isdma", "sem", "val", "needed", "idx")


ENGS = ("pe", "act", "dve", "pool", "sp")
NDMA_SEMS = {"sp": 12, "pool": 8}


class Sched:
    def __init__(self):
        self.q = {e: [] for e in ENGS}
        self.dma_ops = {e: [] for e in NDMA_SEMS}

    def add(self, eng, fn, reads=(), writes=(), dma=False):
        op = Op()
        op.eng = eng
        op.fn = fn
        op.isdma = dma
        op.needed = False
        op.sem = None
        op.val = 0
        if any(b.excl for b in reads):
            writes = list(writes) + [b for b in reads if b.excl]
            reads = [b for b in reads if not b.excl]
        keep = {}
        for b in reads:
            d = b.w
            if d is not None:
                if d.isdma or dma or d.eng != eng or eng != "pe":
                    keep[id(d)] = d
        same_ok = eng == "pe"
        for b in writes:
            d = b.w
            if d is not None and (d.isdma or dma or d.eng != eng or not same_ok):
                keep[id(d)] = d
            for d in b.r:
                if d.isdma or dma or d.eng != eng or not same_ok:
                    keep[id(d)] = d
        op.deps = list(keep.values())
        for b in reads:
            b.r.append(op)
        for b in writes:
            b.w = op
            b.r = []
        if dma:
            op.idx = len(self.dma_ops[eng])
            self.dma_ops[eng].append(op)
        self.q[eng].append(op)
        return op

    def emit(self, block, sems, dma_sems, final_wait_ops=()):
        for e in ENGS:
            for op in self.q[e]:
                for d in op.deps:
                    d.needed = True
        for op in final_wait_ops:
            op.needed = True
        for e in ENGS:
            cnt = 0
            for op in self.q[e]:
                if op.isdma:
                    op.sem = dma_sems[e][op.idx % NDMA_SEMS[e]]
                    op.val = 16 * (op.idx // NDMA_SEMS[e] + 1)
                elif op.needed:
                    cnt += 1
                    op.sem = sems[e]
                    op.val = cnt
        dma_ops = self.dma_ops

        def run(e, eng):
            waited = {}
            for op in self.q[e]:
                waits = {}
                for d in op.deps:
                    key = id(d.sem)
                    if key not in waits or waits[key][1] < d.val:
                        waits[key] = (d.sem, d.val)
                if op.isdma and op.idx >= NDMA_SEMS[e]:
                    prev = dma_ops[e][op.idx - NDMA_SEMS[e]]
                    key = id(prev.sem)
                    if key not in waits or waits[key][1] < prev.val:
                        waits[key] = (prev.sem, prev.val)
                for key, (s, v) in waits.items():
                    if waited.get(key, 0) >= v:
                        continue
                    waited[key] = v
                    eng.wait_ge(s, v)
                ins = op.fn(eng)
                if op.isdma:
                    ins.then_inc(op.sem, 16)
                elif op.needed:
                    ins.then_inc(op.sem, 1)
            if e == "sp":
                for op in final_wait_ops:
                    if waited.get(id(op.sem), 0) < op.val:
                        waited[id(op.sem)] = op.val
                        eng.wait_ge(op.sem, op.val)

        @block.tensor
        def _(eng):
            run("pe", eng)

        @block.scalar
        def _(eng):
            run("act", eng)

        @block.vector
        def _(eng):
            run("dve", eng)

        @block.gpsimd
        def _(eng):
            run("pool", eng)

        @block.sync
        def _(eng):
            run("sp", eng)


_DEBUG_INFO = {}


def _tile_units(first):
    lst = list(range(U_WINF, U_WINF + 12)) + list(range(U_WINT, U_WINT + 8)) + [U_POOLW]
    lst += list(range(U_WOUT, U_WOUT + 8))
    return lst


def build_program(ntiles=SEQ // T, nseq=NSEQ, ring_r=24, interleave=True, x_head=8.0):
    seq = _build(ntiles, nseq, ring_r, interleave, None, x_head)
    return _build(ntiles, nseq, ring_r, interleave, seq, x_head)


def _build(ntiles, nseq, ring_r, interleave, seq_units, x_head=8.0):
    record = seq_units is None
    rec_units = []
    nc = bass.Bass("TRN2", target_bir_lowering=False)
    ntok = nseq * ntiles * T
    x_d = nc.dram_tensor("x", [ntok, D], F32, kind="ExternalInput").ap()
    mem_d = nc.dram_tensor("mem", [nseq * NMEM, D], F32, kind="ExternalInput").ap()
    wall_d = nc.dram_tensor("wall", [NU, 128, 1024], F32, kind="ExternalInput").ap()
    gains_d = nc.dram_tensor("gains", [128, 5, 8], F32, kind="ExternalInput").ap()
    theta_d = nc.dram_tensor("theta", [128, 2, 4], F32, kind="ExternalInput").ap()
    gfin_d = nc.dram_tensor("gfin", [128, 1024], F32, kind="ExternalInput").ap()
    ident_d = nc.dram_tensor("ident", [128, 128], F32, kind="ExternalInput").ap()
    cmask_d = nc.dram_tensor("cmask", [128, 128], F32, kind="ExternalInput").ap()
    pband_d = nc.dram_tensor("pband", [128, 12, 128], F32, kind="ExternalInput").ap()
    out_d = nc.dram_tensor("out", [ntok, D], F32, kind="ExternalOutput").ap()
    wsc_d = nc.dram_tensor("wsc", [NU, 128, 1024], BF16).ap()

    S = Sched()
    with contextlib.ExitStack() as es:
        sb_addr = {}

        def sb(name, shape, d):
            t_ = es.enter_context(nc.sbuf_tensor("sb_" + name, shape, d))
            try:
                sb_addr[name] = (nc.lookup_mloc(t_).addr, int(np.prod(shape[1:])) * (4 if d == F32 else 2))
            except Exception:
                pass
            return t_

        h_t = sb("h", [128, 8 * 1024], F32)
        ring_t = sb("ring", [128, ring_r * 512], BF16)
        xn_t = sb("xn", [128, 4, 1024], BF16)
        ss_t = sb("ss", [128, 20], F32)
        lnv_t = sb("lnv", [128, 20], F32)
        rstd_t = sb("rstd", [128, 20], F32)
        xnTX_t = sb("xnTX", [128, 8, 512], BF16)
        arX_t = sb("arX", [128, 16, 512], BF16)
        arB_t = sb("arB", [128, 2048], F32)
        kT_t = sb("kT", [128, 8, 256], BF16)
        vmem_t = sb("vmem", [128, 2, 1024], BF16)
        rl_t = sb("rl", [128, 2, 512], BF16)
        xnTY_t = sb("xnTY", [128, 8, 512], BF16)
        tmpY_t = sb("tmpY", [128, 4096], F32)
        Acum_t = sb("Acum", [128, 2048], F32)
        og_t = sb("og", [128, 4, 512], BF16)
        krel_t = sb("krel", [128, 4, 512], BF16)
        qG_t = sb("qG", [128, 4, 512], BF16)
        gsil_t = sb("gsil", [128, 4, 512], BF16)
        utok_t = sb("utok", [128, 5, 512], BF16)
        vtok_t = sb("vtok", [128, 4, 512], BF16)
        ktok_t = sb("ktok", [128, 4, 512], BF16)
        sTm_t = sb("sTm", [128, 4, 512], BF16)
        P_t = sb("P", [128, 2, 4, 128], F32)
        Sbf_t = sb("Sbf", [128, 8, 4, 128], BF16)
        dec_t = sb("dec", [128, 4, 9], F32)
        osq_t = sb("osq", [128, 2, 512], BF16)
        lnb_t = sb("lnb", [128, 2, 512], F32)
        mixT_t = sb("mixT", [128, 8, 512], BF16)
        ident_t = sb("ident", [128, 128], BF16)
        mask_t = sb("mask", [128, 128], BF16)
        pband_t = sb("pband", [128, 12, 128], BF16)
        ones_t = sb("ones", [128, 128], BF16)
        cm01_t = sb("cm01", [128, 512], BF16)
        gfin_t = sb("gfin", [128, 1024], F32)
        gains_t = sb("gains", [128, 5, 8], F32)
        theta_t = sb("theta", [128, 2, 4], F32)
        lb_t = sb("lb", [128, 4], F32)
        c1_t = sb("c1", [128, 4], F32)
        nc1_t = sb("nc1", [128, 4], F32)
        banks = [es.enter_context(nc.psum_tensor("pb%d" % i, [128, 512], F32)) for i in range(8)]
        bbank = [Buf("bank%d" % i, excl=True) for i in range(8)]
        sems = {e: es.enter_context(nc.semaphore("s_" + e)) for e in ENGS}
        dsems = {q: [es.enter_context(nc.semaphore("d%s%d" % (q, i))) for i in range(n)] for q, n in NDMA_SEMS.items()}
        block = es.enter_context(nc.Block())

        bank_ctr = [0]

        def next_bank():
            i = bank_ctr[0] % 8
            bank_ctr[0] += 1
            return banks[i], bbank[i]

        def hv(buf, tb):
            return h_t[:, (buf * 4 + tb) * 1024:(buf * 4 + tb + 1) * 1024]

        def hv_all(buf):
            return h_t[:, buf * 4096:(buf + 1) * 4096].rearrange("p (tb d) -> p tb d", tb=4)

        def ringv(slot):
            return ring_t[:, slot * 512:(slot + 1) * 512]

        sigf = lambda i: tmpY_t[:, i * 512:(i + 1) * 512]
        kkv = lambda i: tmpY_t[:, 2048 + i * 512:2048 + (i + 1) * 512]
        qfv = lambda i: tmpY_t[:, 3072 + i * 512:3072 + (i + 1) * 512]
        Av = lambda hh: Acum_t[:, hh * 512:(hh + 1) * 512]
        qT_t = arX_t[:, 0:8, :]
        attnT_t = arX_t[:, 8:16, :]
        aT_all = arX_t[:, :, :].rearrange("p (g j) t -> p g j t", g=2)
        expT_all = arB_t[:, 0:1024].bitcast(BF16).rearrange("p (b m t) -> p b m t", b=2, m=2)
        rdenv = lambda i: arB_t[:, 1024 + i * 512:1024 + (i + 1) * 512]
        memfv = lambda mb: arB_t[:, mb * 1024:(mb + 1) * 1024]
        pbandf = arB_t[:, 0:1536].rearrange("p (a b) -> p a b", a=12)
        identf = arB_t[:, 1536:1664]
        maskf = arB_t[:, 1664:1792]

        b_h = [[Buf("h%d_%d" % (i, tb)) for tb in range(4)] for i in range(2)]
        b_ring = [Buf("ring%d" % i) for i in range(ring_r)]
        b_xn = [Buf("xn%d" % i) for i in range(4)]
        b_xnTX = [Buf("xnTX%d" % tb) for tb in range(4)]
        b_xnTY = [Buf("xnTY%d" % tb) for tb in range(4)]
        b_mixT = [Buf("mixT%d" % c) for c in range(8)]
        b_qT = [Buf("qT%d" % c) for c in range(8)]
        b_attnT = [Buf("attnT%d" % c) for c in range(8)]
        b_arX = b_qT + b_attnT
        b_ss = [Buf("ss%d" % i) for i in range(5)]
        b_lnv = [Buf("lnv%d" % i) for i in range(5)]
        b_rstd = [Buf("rstd%d" % i) for i in range(5)]
        b_sigf = [Buf("sigf%d" % i) for i in range(4)]
        b_kk = [Buf("kk0"), Buf("kk1")]
        b_qf = [Buf("qf0"), Buf("qf1")]
        b_cm01 = Buf("cm01")
        b_A = [Buf("A%d" % i) for i in range(4)]
        b_krel = [Buf("krel%d" % i) for i in range(4)]
        b_qG = [Buf("qG%d" % i) for i in range(4)]
        b_gsil = [Buf("gsil%d" % i) for i in range(4)]
        b_utok = [Buf("utok%d" % i) for i in range(5)]
        b_vtok = [Buf("vtok%d" % i) for i in range(4)]
        b_ktok = [Buf("ktok%d" % i) for i in range(4)]
        b_sTm = [Buf("sTm%d" % i) for i in range(4)]
        b_P = [[Buf("P%d_%d" % (i, hh)) for hh in range(4)] for i in range(2)]
        b_Sbf = [Buf("Sbf%d" % i) for i in range(8)]
        b_dec = Buf("dec")
        b_dec0 = Buf("dec0")
        b_og = [Buf("og%d" % i) for i in range(4)]
        b_osq = [Buf("osq0"), Buf("osq1")]
        b_lnb = [Buf("lnb0"), Buf("lnb1")]
        b_expT = [[Buf("expT%d_%d" % (i, m)) for m in range(2)] for i in range(2)]
        b_rden = [Buf("rden0"), Buf("rden1")]
        b_arB = b_expT[0] + b_expT[1] + b_rden
        b_kT = Buf("kT")
        b_vmem = Buf("vmem")
        b_rl = [Buf("rl0"), Buf("rl1")]
        b_ident = Buf("ident")
        b_mask = Buf("mask")
        b_pband = Buf("pband")
        b_ones = Buf("ones")
        b_gfin = Buf("gfin")
        b_gains = Buf("gains")
        b_theta = Buf("theta")
        b_lbc = Buf("lbc")
        b_wsc = [Buf("wsc%d" % u) for u in range(NU)]

        def A(eng, fn, reads=(), writes=(), dma=False):
            return S.add(eng, fn, reads=reads, writes=writes, dma=dma)

        def mm(out, lhsT, rhs, start, stop, reads, writes):
            A("pe", lambda e: e.matmul(out, lhsT=lhsT, rhs=rhs, start=start, stop=stop), reads, writes)

        def tr(out, in_, reads, writes):
            A("pe", lambda e: e.transpose(out=out, in_=in_, identity=ident_t[:]), reads + [b_ident], writes)

        def act(out, in_, func, reads, writes, scale=None, bias=None, accum_out=None):
            kw = {}
            if scale is not None:
                kw["scale"] = scale
            if bias is not None:
                kw["bias"] = bias
            if accum_out is not None:
                kw["accum_out"] = accum_out
            A("act", lambda e: e.activation(out=out, in_=in_, func=func, **kw), reads, writes)

        def ts(eng, out, in0, s1, s2, op0, op1, reads, writes):
            if s2 is None:
                A(eng, lambda e: e.tensor_scalar(out=out, in0=in0, scalar1=s1, scalar2=None, op0=op0), reads, writes)
            else:
                A(eng, lambda e: e.tensor_scalar(out=out, in0=in0, scalar1=s1, scalar2=s2, op0=op0, op1=op1),
                  reads, writes)

        def tt(eng, out, in0, in1, op, reads, writes):
            A(eng, lambda e: e.tensor_tensor(out=out, in0=in0, in1=in1, op=op), reads, writes)

        def stt(out, in0, scalar, in1, op0, op1, reads, writes):
            A("dve", lambda e: e.scalar_tensor_tensor(out=out, in0=in0, scalar=scalar, in1=in1, op0=op0, op1=op1),
              reads, writes)

        def cp(eng, out, in_, reads, writes):
            if eng == "act":
                act(out, in_, AF.Copy, reads, writes)
            else:
                A(eng, lambda e: e.tensor_copy(out=out, in_=in_), reads, writes)

        def recip(out, in_, reads, writes):
            A("dve", lambda e: e.reciprocal(out=out, in_=in_), reads, writes)

        def dma(out, in_, reads, writes):
            return A("sp", lambda e: e.dma_start(out=out, in_=in_), reads, writes, dma=True)

        dma(identf, ident_d[:, :], [], [b_arB[0]])
        dma(maskf, cmask_d[:, :], [], [b_arB[1]])
        dma(pbandf, pband_d[:, :, :], [], [b_arB[2]])
        dma(gfin_t[:], gfin_d[:, :], [], [b_gfin])
        dma(gains_t[:], gains_d[:, :, :], [], [b_gains])
        dma(theta_t[:], theta_d[:, :, :], [], [b_theta])
        cp("dve", ident_t[:], identf, [b_arB[0]], [b_ident])
        cp("dve", mask_t[:], maskf, [b_arB[1]], [b_mask])
        cp("dve", pband_t[:], pbandf, [b_arB[2]], [b_pband])
        A("dve", lambda e: e.memset(ones_t[:], 1.0), [], [b_ones])
        tt("dve", lb_t[:], theta_t[:, 0, :], theta_t[:, 1, :], ALU.subtract, [b_theta], [b_lbc])
        act(lb_t[:], lb_t[:], AF.Sigmoid, [b_lbc], [b_lbc])
        ts("dve", c1_t[:], lb_t[:], -1.0, 1.0, ALU.mult, ALU.add, [b_lbc], [b_lbc])
        ts("dve", nc1_t[:], c1_t[:], -1.0, None, ALU.mult, None, [b_lbc], [b_lbc])
        A("pool", lambda e: e.memset(cm01_t[:], 1.0), [], [b_cm01])
        A("pool", lambda e: e.memset(cm01_t[:, 0:512:64], 0.0), [], [b_cm01])

        conv_order = []
        if not record:
            seen = set()
            for u, _hf in seq_units:
                if u not in seen:
                    seen.add(u)
                    conv_order.append(u)
            assert len(conv_order) == NU
        conv_state = {"out": 0}
        conv_pos = {u: i for i, u in enumerate(conv_order)}

        def conv_upto(k_target):
            k_target = min(k_target, NU - 1)
            while conv_state["out"] <= k_target:
                u = conv_order[conv_state["out"]]
                gate = (b_h[0][0], b_h[0][1], b_h[0][2], b_h[0][3]) if conv_state["out"] == 0 else ()
                A("pool", lambda e, u=u: e.dma_start(out=wsc_d[u, :, :], in_=wall_d[u, :, :]), list(gate), [b_wsc[u]],
                  dma=True)
                conv_state["out"] += 1

        ring_state = {"loaded": 0, "cur": 0}
        CONV_AHEAD = 10
        free_slots = list(range(ring_r))
        slot_of = {}

        def ring_fill():
            while free_slots and ring_state["loaded"] < len(seq_units):
                n = ring_state["loaded"]
                slot = free_slots.pop(0)
                u, hf = seq_units[n]
                conv_upto(conv_pos[u] + CONV_AHEAD)
                dma(ringv(slot), wsc_d[u, :, hf * 512:(hf + 1) * 512], [b_wsc[u]], [b_ring[slot]])
                slot_of[n] = slot
                ring_state["loaded"] += 1

        def ring_next(u, hf):
            n = ring_state["cur"]
            ring_state["cur"] += 1
            if record:
                rec_units.append((u, hf))
                return ringv(0), b_ring[0], n
            assert seq_units[n] == (u, hf), (n, seq_units[n], (u, hf))
            assert n < ring_state["loaded"], "weight ring exhausted (too many half-units held)"
            slot = slot_of[n]
            return ringv(slot), b_ring[slot], n

        def ring_done(n):
            if record:
                return
            free_slots.append(slot_of.pop(n))
            ring_fill()

        MMUS = 0.216
        import os as _os
        NORM_Y1A = float(_os.environ.get("KN_Y1A", "12"))
        NORM_Y1C = float(_os.environ.get("KN_Y1C", "12"))

        def norm_square(hbuf, grp, tb, junk, b_junk):
            act(junk, hv(hbuf, tb), AF.Square, [b_h[hbuf][tb]], b_junk + [b_ss[grp]],
                accum_out=ss_t[:, grp * 4 + tb:grp * 4 + tb + 1])

        def rms_norm_T(hbuf, grp, xnT_t, b_xnT, gi, strm, junk, b_junk, y1=7.0, y3=2.0, squares_done=False):
            sl = slice(grp * 4, grp * 4 + 4)
            if not squares_done:
                for tb in range(4):
                    norm_square(hbuf, grp, tb, junk, b_junk)
            act(lnv_t[:, sl], ss_t[:, sl], AF.Ln, [b_ss[grp]], [b_lnv[grp]], scale=1.0 / D, bias=EPS)
            act(rstd_t[:, sl], lnv_t[:, sl], AF.Exp, [b_lnv[grp]], [b_rstd[grp]], scale=-0.5)

            def scale(tb):
                si = grp * 4 + tb
                xb = strm * 2 + tb % 2
                act(xn_t[:, xb, :], hv(hbuf, tb), AF.Copy, [b_h[hbuf][tb], b_rstd[grp]], [b_xn[xb]],
                    scale=rstd_t[:, si:si + 1])

            def transp(tb):
                xb = strm * 2 + tb % 2
                bk, bb = next_bank()
                bkb = bk[:].bitcast(BF16)
                for kc in range(8):
                    tr(bkb[:, kc * 128:(kc + 1) * 128], xn_t[:, xb, kc * 128:(kc + 1) * 128], [b_xn[xb]], [bb])
                tt("dve", xnT_t[:, :, tb * 128:(tb + 1) * 128], bkb.rearrange("p (k m) -> p k m", k=8),
                   gains_t[:, gi, :].unsqueeze(2).to_broadcast([128, 8, 128]), ALU.mult, [bb, b_gains], [b_xnT[tb]])

            scale(0)
            scale(1)
            yield y1
            transp(0)
            transp(1)
            scale(2)
            scale(3)
            yield 3.5
            transp(2)
            transp(3)
            yield y3

        def proj_tokmajor(hbuf, actT, b_act, unit_base, after_tb=None):
            for half in range(2):
                us = [ring_next(unit_base + kc, half) for kc in range(8)]
                for tb in range(4):
                    bk, bb = next_bank()
                    for kc in range(8):
                        mm(bk[:], actT[:, kc, tb * 128:(tb + 1) * 128], us[kc][0], kc == 0, kc == 7,
                           [b_act[kc], us[kc][1]], [bb])
                        if tb == 3:
                            ring_done(us[kc][2])
                    hs = hv(hbuf, tb)[:, half * 512:(half + 1) * 512]
                    tt("dve", hs, bk[:], hs, ALU.add, [bb, b_h[hbuf][tb]], [b_h[hbuf][tb]])
                    if half == 1 and after_tb is not None:
                        after_tb(tb)
                    yield 8 * MMUS

        def feat_group(unit_id, xnT_t, b_xnT, n=512):
            hu = [ring_next(unit_id, 0), ring_next(unit_id, 1)]
            bk, bb = next_bank()
            for kc in range(8):
                ut, ub, un = hu[kc // 4]
                k4 = kc % 4
                mm(bk[:, 0:n], ut[:, k4 * 128:(k4 + 1) * 128], xnT_t[:, kc, 0:n], kc == 0, kc == 7, [ub] + b_xnT, [bb])
                if k4 == 3:
                    ring_done(un)
            return bk, bb

        def phaseA(g, s_i, t_i):
            hb = g % 2
            first = t_i == 0
            yield from rms_norm_T(hb, 2, xnTY_t, b_xnTY, 0, 1, osq_t[:, :, :].rearrange("p a b -> p (a b)"), b_osq, y1=NORM_Y1A, y3=4.0)
            if first:
                for i in range(2):
                    for hh in range(4):
                        A("pool", lambda e, i=i, hh=hh: e.memset(P_t[:, i, hh, :], 0.0), [], [b_P[i][hh]])
                A("pool", lambda e: e.memset(dec_t[:, :, 0:1], 1.0), [], [b_dec0])
            for hh in range(4):
                bk, bb = feat_group(U_WINF + hh, xnTY_t, b_xnTY)
                act(sigf(hh), bk[:], AF.Sigmoid, [bb], [b_sigf[hh]])
            for hh in range(4):
                i2 = hh % 2
                ts("dve", kkv(i2), sigf(hh), nc1_t[:, hh:hh + 1], c1_t[:, hh:hh + 1], ALU.mult, ALU.add,
                   [b_sigf[hh], b_lbc], [b_kk[i2]])
                act(sigf(hh), sigf(hh), AF.Ln, [b_sigf[hh], b_lbc], [b_sigf[hh]], scale=c1_t[:, hh:hh + 1],
                    bias=lb_t[:, hh:hh + 1])
                A("dve", lambda e, hh=hh: e.tensor_tensor_scan(
                    out=Av(hh), data0=cm01_t[:], data1=sigf(hh), initial=0.0, op0=ALU.mult, op1=ALU.add),
                  [b_sigf[hh], b_cm01], [b_A[hh]])
                act(dec_t[:, hh, 1:9], Av(hh)[:, 63:512:64], AF.Exp, [b_A[hh]], [b_dec])
                act(sigf(hh), Av(hh), AF.Exp, [b_A[hh]], [b_sigf[hh]], scale=-1.0)
                act(Av(hh), Av(hh), AF.Exp, [b_A[hh]], [b_A[hh]])
                tt("pool", krel_t[:, hh, :], kkv(i2), sigf(hh), ALU.mult, [b_kk[i2], b_sigf[hh]], [b_krel[hh]])
            for hh in range(4):
                bk, bb = feat_group(U_WINF + 4 + hh, xnTY_t, b_xnTY)
                i2 = hh % 2
                act(qfv(i2), bk[:], AF.Silu, [bb], [b_qf[i2]])
                tt("pool", qG_t[:, hh, :], qfv(i2), Av(hh), ALU.mult, [b_qf[i2], b_A[hh]], [b_qG[hh]])
            for hh in range(4):
                bk, bb = feat_group(U_WINF + 8 + hh, xnTY_t, b_xnTY)
                act(gsil_t[:, hh, :], bk[:], AF.Silu, [bb], [b_gsil[hh]])
            yield 32.0
            us = [ring_next(U_WINT + kc, 1) for kc in range(8)]
            for tb in range(4):
                bk, bb = next_bank()
                for kc in range(8):
                    mm(bk[:], xnTY_t[:, kc, tb * 128:(tb + 1) * 128], us[kc][0], kc == 0, kc == 7,
                       [b_xnTY[tb], us[kc][1]], [bb])
                    if tb == 3:
                        ring_done(us[kc][2])
                cp("act", vtok_t[:, tb, :], bk[:], [bb], [b_vtok[tb]])
                yield 8 * MMUS
            us = [ring_next(U_WINT + kc, 0) for kc in range(8)]
            for tb in range(4):
                bk, bb = next_bank()
                for kc in range(8):
                    mm(bk[:], xnTY_t[:, kc, tb * 128:(tb + 1) * 128], us[kc][0], kc == 0, kc == 7,
                       [b_xnTY[tb], us[kc][1]], [bb])
                    if tb == 3:
                        ring_done(us[kc][2])
                cp("dve", utok_t[:, tb + 1, :], bk[:], [bb], [b_utok[tb + 1]])
                yield 8 * MMUS
            pws = ring_next(U_POOLW, 0)

            def pool_lin(gq):
                bk2, bb2 = next_bank()
                mm(bk2[:], pws[0][:, gq * 128:(gq + 1) * 128], og_t[:, gq, :], True, True, [pws[1], b_og[gq]], [bb2])
                ts("dve", mixT_t[:, gq, :], bk2[:], gains_t[:, 4, gq:gq + 1], None, ALU.mult, None, [bb2, b_gains],
                   [b_mixT[gq]])

            for gq in range(4):
                bk, bb = next_bank()
                for tb in range(4):
                    o_ = bk[:, tb * 128:(tb + 1) * 128]
                    cur = utok_t[:, tb + 1, gq * 128:(gq + 1) * 128]
                    prv = utok_t[:, tb, gq * 128:(gq + 1) * 128]
                    if first and tb == 0:
                        mm(o_, cur, pband_t[:, 8 + gq, :], True, True, [b_utok[1], b_pband], [bb])
                    else:
                        mm(o_, cur, pband_t[:, gq, :], True, False, [b_utok[tb + 1], b_pband], [bb])
                        mm(o_, prv, pband_t[:, 4 + gq, :], False, True, [b_utok[tb], b_pband], [bb])
                cp("act", og_t[:, gq, :], bk[:], [bb], [b_og[gq]])
                if gq > 0:
                    pool_lin(gq - 1)
                yield 2.5
            pool_lin(3)
            ring_done(pws[2])
            cp("pool", utok_t[:, 0, :], utok_t[:, 4, :], [b_utok[4]], [b_utok[0]])
            for tb in range(4):
                bk, bb = next_bank()
                bkb = bk[:].bitcast(BF16)
                for hh in range(4):
                    tr(bkb[:, hh * 128:(hh + 1) * 128], krel_t[:, hh, tb * 128:(tb + 1) * 128], [b_krel[hh]], [bb])
                cp("act", ktok_t[:, tb, :], bkb[:, 0:512], [bb], [b_ktok[tb]])
                bk, bb = next_bank()
                for hh in range(4):
                    mm(bk[:, hh * 128:(hh + 1) * 128], krel_t[:, hh, tb * 128:(tb + 1) * 128],
                       qG_t[:, hh, tb * 128:(tb + 1) * 128], True, True, [b_krel[hh], b_qG[hh]], [bb])
                tt("dve", sTm_t[:, tb, :].rearrange("p (h t) -> p h t", h=4),
                   bk[:].rearrange("p (h t) -> p h t", h=4),
                   mask_t[:].unsqueeze(1).to_broadcast([128, 4, 128]), ALU.mult, [bb, b_mask], [b_sTm[tb]])
                yield 2.0
                for half in range(2):
                    c = 2 * tb + half
                    pi = c % 2
                    tt("pool", Sbf_t[:, c, :, :], P_t[:, pi, :, :],
                       dec_t[:, :, c:c + 1].to_broadcast([128, 4, 128]), ALU.mult,
                       [b_P[pi][0], b_P[pi][1], b_P[pi][2], b_P[pi][3], b_dec, b_dec0], [b_Sbf[c]])
                    bk, bb = next_bank()
                    ps = slice(half * 64, half * 64 + 64)
                    for hh in range(4):
                        mm(bk[:, hh * 128:(hh + 1) * 128], ktok_t[ps, tb, hh * 128:(hh + 1) * 128],
                           vtok_t[ps, tb, hh * 128:(hh + 1) * 128], True, True, [b_ktok[tb], b_vtok[tb]], [bb])
                    for hh in range(4):
                        stt(P_t[:, 1 - pi, hh, :], P_t[:, pi, hh, :], dec_t[:, hh, c:c + 1],
                            bk[:, hh * 128:(hh + 1) * 128], ALU.mult, ALU.add,
                            [b_P[pi][hh], b_dec, b_dec0, bb], [b_P[1 - pi][hh]])
                    yield 2.0
            cp("dve", dec_t[:, :, 0:1], dec_t[:, :, 8:9], [b_dec], [b_dec0])
            def hgrn_fin(hh):
                o2 = hh % 2
                bk2, bb2 = next_bank()
                mm(bk2[:], ones_t[:], osq_t[:, o2, :], True, True, [b_ones, b_osq[o2]], [bb2])
                act(lnb_t[:, o2, :], bk2[:], AF.Ln, [bb2], [b_lnb[o2]], scale=1.0 / 128.0, bias=EPS)
                act(lnb_t[:, o2, :], lnb_t[:, o2, :], AF.Exp, [b_lnb[o2]], [b_lnb[o2]], scale=-0.5)
                stt(mixT_t[:, 4 + hh, :], og_t[:, hh, :], gains_t[:, 4, 4 + hh:5 + hh], lnb_t[:, o2, :], ALU.mult, ALU.mult,
                    [b_og[hh], b_lnb[o2], b_gains], [b_mixT[4 + hh]])

            for hh in range(4):
                bk, bb = next_bank()
                for tb in range(4):
                    mm(bk[:, tb * 128:(tb + 1) * 128], vtok_t[:, tb, hh * 128:(hh + 1) * 128],
                       sTm_t[:, tb, hh * 128:(hh + 1) * 128], True, False, [b_vtok[tb], b_sTm[tb]], [bb])
                    for half in range(2):
                        c = 2 * tb + half
                        cs = slice(tb * 128 + half * 64, tb * 128 + half * 64 + 64)
                        mm(bk[:, cs], Sbf_t[:, c, hh, :], qG_t[:, hh, cs], False, half == 1,
                           [b_Sbf[c], b_qG[hh]], [bb])
                o2 = hh % 2
                act(osq_t[:, o2, :], bk[:], AF.Square, [bb], [b_osq[o2]])
                tt("dve", og_t[:, hh, :], bk[:], gsil_t[:, hh, :], ALU.mult, [bb, b_gsil[hh]], [b_og[hh]])
                if hh > 0:
                    hgrn_fin(hh - 1)
                yield 5.0
            hgrn_fin(3)
            yield 2.0
            junkY = osq_t[:, :, :].rearrange("p a b -> p (a b)")
            yield from proj_tokmajor(hb, mixT_t, b_mixT, U_WOUT,
                                     after_tb=lambda tb: norm_square(hb, 4, tb, junkY, b_osq))
            yield from rms_norm_T(hb, 4, xnTY_t, b_xnTY, 1, 1, junkY, b_osq, squares_done=True)

        memn_half = arB_t[:, 0:1024]
        memnT_v = arB_t[:, 1024:2048].bitcast(BF16).rearrange("p (k m) -> p k m", k=8)
        b_memf = b_expT[0] + b_expT[1]

        def kv_gen(s_i, first_block_loaded=False):
            junk = rl_t[:, :, :].rearrange("p a b -> p (a b)")
            for mb in range(2):
                if not (mb == 0 and first_block_loaded):
                    dma(memn_half, mem_d[s_i * NMEM + mb * 128:s_i * NMEM + (mb + 1) * 128, :], [], b_memf)
                act(junk, memn_half, AF.Square, b_memf, b_rl + [b_ss[1]], accum_out=ss_t[:, 4 + mb:5 + mb])
                act(lnv_t[:, 4 + mb:5 + mb], ss_t[:, 4 + mb:5 + mb], AF.Ln, [b_ss[1]], [b_lnv[1]], scale=1.0 / D, bias=EPS)
                act(rstd_t[:, 4 + mb:5 + mb], lnv_t[:, 4 + mb:5 + mb], AF.Exp, [b_lnv[1]], [b_rstd[1]], scale=-0.5)
                act(xn_t[:, mb, :], memn_half, AF.Copy, b_memf + [b_rstd[1]], [b_xn[mb]], scale=rstd_t[:, 4 + mb:5 + mb])
                yield 6.0
                bk, bb = next_bank()
                bkb = bk[:].bitcast(BF16)
                for kc in range(8):
                    tr(bkb[:, kc * 128:(kc + 1) * 128], xn_t[:, mb, kc * 128:(kc + 1) * 128], [b_xn[mb]], [bb])
                tt("dve", memnT_v[:, :, mb * 128:(mb + 1) * 128], bkb.rearrange("p (k m) -> p k m", k=8),
                   gains_t[:, 2, :].unsqueeze(2).to_broadcast([128, 8, 128]), ALU.mult, [bb, b_gains], b_rden)
                yield 2.0
            for c in range(8):
                bk, bb = feat_group(U_XK + c, memnT_v, b_rden, n=256)
                cp("act" if c % 2 == 0 else "dve", kT_t[:, c, :], bk[:, 0:256], [bb], [b_kT])
                yield 8 * 0.11
            for half in range(2):
                us = [ring_next(U_XV + kc, half) for kc in range(8)]
                for mb in range(2):
                    bk, bb = next_bank()
                    for kc in range(8):
                        mm(bk[:], memnT_v[:, kc, mb * 128:(mb + 1) * 128], us[kc][0], kc == 0, kc == 7,
                           b_rden + [us[kc][1]], [bb])
                        if mb == 1:
                            ring_done(us[kc][2])
                    cp("act" if half == 0 else "dve", vmem_t[:, mb, half * 512:(half + 1) * 512], bk[:],
                       [bb], [b_vmem])
                    yield 8 * MMUS

        def phaseX(g, s_i, t_i):
            hb = g % 2
            for c in range(8):
                bk, bb = feat_group(U_XQ + c, xnTY_t, b_xnTY)
                if c % 2 == 0:
                    act(qT_t[:, c, :], bk[:], AF.Copy, [bb], [b_qT[c]], scale=1.0 / 16.0)
                else:
                    ts("dve", qT_t[:, c, :], bk[:], 1.0 / 16.0, None, ALU.mult, None, [bb], [b_qT[c]])
                yield 8 * MMUS
            yield -1.0
            def attn_scores(a):
                e2 = a % 2
                for mc in range(2):
                    bk, bb = next_bank()
                    for j in range(2):
                        mm(bk[:], kT_t[:, 2 * a + j, mc * 128:(mc + 1) * 128], qT_t[:, 2 * a + j, :], j == 0, j == 1,
                           [b_kT, b_qT[2 * a + j]], [bb])
                    act(expT_all[:, e2, mc, :], bk[:], AF.Exp, [bb], [b_expT[e2][mc]])

            def attn_pv(a):
                e2 = a % 2
                bk, bb = next_bank()
                for mc in range(2):
                    mm(bk[:], ones_t[:], expT_all[:, e2, mc, :], mc == 0, mc == 1, [b_ones, b_expT[e2][mc]], [bb])
                act(rdenv(e2), bk[:], AF.Ln, [bb], [b_rden[e2]])
                act(rdenv(e2), rdenv(e2), AF.Exp, [b_rden[e2]], [b_rden[e2]], scale=-1.0)
                for j in range(2):
                    bk, bb = next_bank()
                    for mc in range(2):
                        mm(bk[:], vmem_t[:, mc, (2 * a + j) * 128:(2 * a + j + 1) * 128], expT_all[:, e2, mc, :],
                           mc == 0, mc == 1, [b_vmem, b_expT[e2][mc]], [bb])
                    tt("dve", attnT_t[:, 2 * a + j, :], bk[:], rdenv(e2), ALU.mult, [bb, b_rden[e2]],
                       [b_attnT[2 * a + j]])

            attn_scores(0)
            yield 3.0
            for a in range(1, 4):
                attn_scores(a)
                yield 3.0
                attn_pv(a - 1)
                yield 5.0
            attn_pv(3)
            yield 4.0
            junkX = rl_t[:, :, :].rearrange("p a b -> p (a b)")
            yield from proj_tokmajor(hb, attnT_t, b_attnT, U_XO,
                                     after_tb=lambda tb: norm_square(hb, 0, tb, junkX, b_rl))
            if t_i == ntiles - 1 and s_i + 1 < nseq:
                yield from kv_gen(s_i + 1)
            yield from rms_norm_T(hb, 0, xnTX_t, b_xnTX, 3, 0, junkX, b_rl, y1=NORM_Y1C, y3=4.0, squares_done=True)
            for gq in range(4):
                gb = gq % 2
                for jj in range(8):
                    bk, bb = feat_group(U_WUP + 8 * gq + jj, xnTX_t, b_xnTX)
                    r2 = jj % 2
                    act(rl_t[:, r2, :], bk[:], AF.Relu, [bb], [b_rl[r2]])
                    tt("pool", aT_all[:, gb, jj, :], rl_t[:, r2, :], rl_t[:, r2, :], ALU.mult, [b_rl[r2]],
                       [b_arX[8 * gb + jj]])
                    yield 8 * MMUS
                for half in range(2):
                    us = [ring_next(U_WDN + 8 * gq + jj, half) for jj in range(8)]
                    for tb in range(4):
                        bk, bb = next_bank()
                        for jj in range(8):
                            mm(bk[:], aT_all[:, gb, jj, tb * 128:(tb + 1) * 128], us[jj][0], jj == 0, jj == 7,
                               [b_arX[8 * gb + jj], us[jj][1]], [bb])
                            if tb == 3:
                                ring_done(us[jj][2])
                        hs = hv(hb, tb)[:, half * 512:(half + 1) * 512]
                        tt("dve", hs, bk[:], hs, ALU.add, [bb, b_h[hb][tb]], [b_h[hb][tb]])
                        if gq == 3 and half == 1:
                            norm_square(hb, 3, tb, junkX, b_rl)
                        yield 8 * MMUS
            act(lnv_t[:, 12:16], ss_t[:, 12:16], AF.Ln, [b_ss[3]], [b_lnv[3]], scale=1.0 / D, bias=EPS)
            act(rstd_t[:, 12:16], lnv_t[:, 12:16], AF.Exp, [b_lnv[3]], [b_rstd[3]], scale=-0.5)
            for tb in range(4):
                stt(hv(hb, tb), hv(hb, tb), rstd_t[:, 12 + tb:13 + tb], gfin_t[:], ALU.mult, ALU.mult,
                    [b_h[hb][tb], b_rstd[3], b_gfin], [b_h[hb][tb]])
                if tb % 2 == 1:
                    hf = tb // 2
                    st = dma(out_d[g * T + hf * 256:g * T + (hf + 1) * 256, :].rearrange("(tb p) d -> p tb d", p=128),
                             hv_all(hb)[:, 2 * hf:2 * hf + 2, :], b_h[hb][2 * hf:2 * hf + 2], [])
                    out_stores.append(st)
            yield 8.0

        out_stores = []
        ntot = nseq * ntiles

        def load_x(g):
            buf = g % 2
            for hf in range(2):
                dma(hv_all(buf)[:, 2 * hf:2 * hf + 2, :],
                    x_d[g * T + hf * 256:g * T + (hf + 1) * 256, :].rearrange("(tb p) d -> p tb d", p=128),
                    [], b_h[buf][2 * hf:2 * hf + 2])

        def drive(gx, gy, head_start=0.0, gate=True, x_delay=0.0):
            tx = x_delay
            ty = None if (gx is not None and gate) else 0.0
            while gx is not None or gy is not None:
                if gx is not None and (gy is None or ty is None or tx <= ty):
                    try:
                        c = next(gx)
                        if c < 0:
                            if ty is None:
                                ty = tx + head_start
                        else:
                            tx += c
                    except StopIteration:
                        gx = None
                        if ty is None:
                            ty = tx
                else:
                    try:
                        ty += next(gy)
                    except StopIteration:
                        gy = None

        def coords(g):
            return g, g // ntiles, g % ntiles

        load_x(0)
        if not record:
            ring_fill()
        if ntot > 1:
            load_x(1)
        drive(kv_gen(0), phaseA(*coords(0)), gate=False, x_delay=70.0)
        for g in range(ntot):
            gy = phaseA(*coords(g + 1)) if g + 1 < ntot else None
            if interleave:
                drive(phaseX(*coords(g)), gy, head_start=x_head)
            else:
                drive(phaseX(*coords(g)), None)
                drive(None, gy)
            if g + 2 < ntot:
                load_x(g + 2)
        if not record:
            conv_upto(NU - 1)

        if record:
            return rec_units
        _DEBUG_INFO["sb_addr"] = dict(sb_addr)
        S.emit(block, sems, dsems, final_wait_ops=out_stores)
    return nc


_CACHE = {}


def kernel(**inputs):
    x = np.asarray(inputs["x"], np.float32)
    mem = np.asarray(inputs["mem"], np.float32)
    B = x.shape[0]
    seq = x.shape[1]
    ntiles = seq // T
    nseq = B // NCORES
    wall, gains, theta, gfin = _host_layout(inputs)
    ident, cmask, pband = _host_consts()
    key = (ntiles, nseq)
    if key not in _CACHE:
        _CACHE[key] = build_program(ntiles=ntiles, nseq=nseq)
    nc = _CACHE[key]
    in_maps = []
    for c in range(NCORES):
        in_maps.append({
            "x": np.ascontiguousarray(x[c * nseq:(c + 1) * nseq].reshape(nseq * seq, D)),
            "mem": np.ascontiguousarray(mem[c * nseq:(c + 1) * nseq].reshape(nseq * NMEM, D)),
            "wall": wall, "gains": gains, "theta": theta, "gfin": gfin,
            "ident": ident, "cmask": cmask, "pband": pband,
        })
    res = run_bass_kernel_spmd(nc, in_maps, core_ids=list(range(NCORES)))
    outs = [np.asarray(r["out"]).reshape(nseq, seq, D) for r in res.results]
    return np.concatenate(outs, axis=0).astype(np.float32)
```

```python
import contextlib
import os as _os2

import numpy as np
import concourse.bass as bass
import concourse.mybir as mybir
from concourse.bass_utils import run_bass_kernel_spmd

dt = mybir.dt
F32 = dt.float32
BF16 = dt.bfloat16
AF = mybir.ActivationFunctionType
ALU = mybir.AluOpType

NCORES = 8
D = 1024
SEQ = 4096
T = 512
NSEQ = 2
NMEM = 256
EPS = 1e-6

U_WINF = 0
U_WINT = 12
U_POOLW = 20
U_WOUT = 21
U_XQ = 29
U_XO = 37
U_WUP = 45
U_WDN = 77
U_XK = 109
U_XV = 117
NU = 125


def _fm(w, c):
    blk = w[:, c * 128:(c + 1) * 128]
    return blk.reshape(8, 128, 128).transpose(1, 0, 2).reshape(128, 1024)


def _host_layout(inp):
    w_in = np.asarray(inp["w_in"], np.float32)[0]
    wall = np.zeros((NU, 128, 1024), np.float32)
    for hh in range(4):
        wall[U_WINF + hh] = _fm(w_in, 8 + hh)
        wall[U_WINF + 4 + hh] = _fm(w_in, 4 + hh)
        wall[U_WINF + 8 + hh] = _fm(w_in, 16 + hh)
    for kc in range(8):
        wall[U_WINT + kc, :, 0:512] = w_in[kc * 128:(kc + 1) * 128, 0:512]
        wall[U_WINT + kc, :, 512:1024] = w_in[kc * 128:(kc + 1) * 128, 1536:2048]
    pw = np.asarray(inp["pool_w"], np.float32)[0]
    for g in range(4):
        wall[U_POOLW, :, g * 128:(g + 1) * 128] = pw[g]
    w_out = np.asarray(inp["w_out"], np.float32)[0]
    xw_q = np.asarray(inp["xw_q"], np.float32)[0]
    xw_o = np.asarray(inp["xw_o"], np.float32)[0]
    xw_kv = np.asarray(inp["xw_kv"], np.float32)[0]
    w_up = np.asarray(inp["w_up"], np.float32)[0]
    w_dn = np.asarray(inp["w_down"], np.float32)[0]
    for kc in range(8):
        wall[U_WOUT + kc] = w_out[kc * 128:(kc + 1) * 128]
        wall[U_XO + kc] = xw_o[kc * 128:(kc + 1) * 128]
        wall[U_XQ + kc] = _fm(xw_q, kc)
        wall[U_XK + kc] = _fm(xw_kv, kc)
        wall[U_XV + kc] = xw_kv[kc * 128:(kc + 1) * 128, 1024:2048]
    for j in range(32):
        wall[U_WUP + j] = _fm(w_up, j)
        wall[U_WDN + j] = w_dn[j * 128:(j + 1) * 128]

    def pk(v):
        return np.asarray(v, np.float32).reshape(8, 128).T

    gains = np.zeros((128, 5, 8), np.float32)
    gains[:, 0] = pk(inp["norm_mix"][0])
    gains[:, 1] = pk(inp["norm_xq"][0])
    gains[:, 2] = pk(inp["norm_mem"][0])
    gains[:, 3] = pk(inp["norm_mlp"][0])
    gains[:, 4] = pk(np.concatenate([np.asarray(inp["pool_scale"], np.float32)[0],
                                     np.asarray(inp["hgrn_norm"], np.float32)[0]]))
    th = np.asarray(inp["lb_theta"], np.float32)
    theta = np.ascontiguousarray(th.reshape(2, 4, 128).transpose(2, 0, 1))
    gfin = np.ascontiguousarray(np.broadcast_to(np.asarray(inp["norm_final"], np.float32), (128, 1024)))
    return wall, np.ascontiguousarray(gains), theta, gfin


def _host_consts():
    ident = np.eye(128, dtype=np.float32)
    s = np.arange(128)[:, None]
    t = np.arange(128)[None, :]
    cmask = ((s // 64 == t // 64) & (s <= t)).astype(np.float32)
    pband = np.zeros((128, 12, 128), np.float32)
    for g, w in enumerate((2, 4, 8, 16)):
        inwin = ((t - s) >= 0) & ((t - s) < w)
        pband[:, g, :] = inwin / w - (s == t)
        d = t + 128 - s
        pband[:, 4 + g, :] = ((d >= 0) & (d < w)) / w
        cnt = np.minimum(t + 1, w)
        pband[:, 8 + g, :] = inwin / cnt - (s == t)
    return ident, cmask, pband


class Buf:
    __slots__ = ("name", "w", "r", "excl")

    def __init__(self, name="", excl=False):
        self.name = name
        self.w = None
        self.r = []
        self.excl = excl


class Op:
    __slots__ = ("eng", "fn", "deps", "isdma", "sem", "val", "needed", "idx")


ENGS = ("pe", "act", "dve", "pool", "sp")
NDMA_SEMS = {"sp": 12, "pool": 8}


class Sched:
    def __init__(self):
        self.q = {e: [] for e in ENGS}
        self.dma_ops = {e: [] for e in NDMA_SEMS}

    def add(self, eng, fn, reads=(), writes=(), dma=False):
        op = Op()
        op.eng = eng
        op.fn = fn
        op.isdma = dma
        op.needed = False
        op.sem = None
        op.val = 0
        if any(b.excl for b in reads):
            writes = list(writes) + [b for b in reads if b.excl]
            reads = [b for b in reads if not b.excl]
        keep = {}
        for b in reads:
            d = b.w
            if d is not None:
                if d.isdma or dma or d.eng != eng or eng != "pe":
                    keep[id(d)] = d
        same_ok = eng == "pe"
        for b in writes:
            d = b.w
            if d is not None and (d.isdma or dma or d.eng != eng or not same_ok):
                keep[id(d)] = d
            for d in b.r:
                if d.isdma or dma or d.eng != eng or not same_ok:
                    keep[id(d)] = d
        op.deps = list(keep.values())
        for b in reads:
            b.r.append(op)
        for b in writes:
            b.w = op
            b.r = []
        if dma:
            op.idx = len(self.dma_ops[eng])
            self.dma_ops[eng].append(op)
        self.q[eng].append(op)
        return op

    def emit(self, block, sems, dma_sems, final_wait_ops=()):
        for e in ENGS:
            for op in self.q[e]:
                for d in op.deps:
                    d.needed = True
        for op in final_wait_ops:
            op.needed = True
        for e in ENGS:
            cnt = 0
            for op in self.q[e]:
                if op.isdma:
                    op.sem = dma_sems[e][op.idx % NDMA_SEMS[e]]
                    op.val = 16 * (op.idx // NDMA_SEMS[e] + 1)
                elif op.needed:
                    cnt += 1
                    op.sem = sems[e]
                    op.val = cnt
        dma_ops = self.dma_ops

        def run(e, eng):
            waited = {}
            for op in self.q[e]:
                waits = {}
                for d in op.deps:
                    key = id(d.sem)
                    if key not in waits or waits[key][1] < d.val:
                        waits[key] = (d.sem, d.val)
                if op.isdma and op.idx >= NDMA_SEMS[e]:
                    prev = dma_ops[e][op.idx - NDMA_SEMS[e]]
                    key = id(prev.sem)
                    if key not in waits or waits[key][1] < prev.val:
                        waits[key] = (prev.sem, prev.val)
                for key, (s, v) in waits.items():
                    if waited.get(key, 0) >= v:
                        continue
                    waited[key] = v
                    eng.wait_ge(s, v)
                ins = op.fn(eng)
                if op.isdma:
                    ins.then_inc(op.sem, 16)
                elif op.needed:
                    ins.then_inc(op.sem, 1)
            if e == "sp":
                for op in final_wait_ops:
                    if waited.get(id(op.sem), 0) < op.val:
                        waited[id(op.sem)] = op.val
                        eng.wait_ge(op.sem, op.val)

        @block.tensor
        def _(eng):
            run("pe", eng)

        @block.scalar
        def _(eng):
            run("act", eng)

        @block.vector
        def _(eng):
            run("dve", eng)

        @block.gpsimd
        def _(eng):
            run("pool", eng)

        @block.sync
        def _(eng):
            run("sp", eng)


_DEBUG_INFO = {}


def _tile_units(first):
    lst = list(range(U_WINF, U_WINF + 12)) + list(range(U_WINT, U_WINT + 8)) + [U_POOLW]
    lst += list(range(U_WOUT, U_WOUT + 8))
    return lst


def build_program(ntiles=SEQ // T, nseq=NSEQ, ring_r=24, interleave=True, x_head=8.0):
    seq = _build(ntiles, nseq, ring_r, interleave, None, x_head)
    return _build(ntiles, nseq, ring_r, interleave, seq, x_head)


def _build(ntiles, nseq, ring_r, interleave, seq_units, x_head=8.0):
    record = seq_units is None
    rec_units = []
    nc = bass.Bass("TRN2", target_bir_lowering=False)
    ntok = nseq * ntiles * T
    x_d = nc.dram_tensor("x", [ntok, D], F32, kind="ExternalInput").ap()
    mem_d = nc.dram_tensor("mem", [nseq * NMEM, D], F32, kind="ExternalInput").ap()
    wall_d = nc.dram_tensor("wall", [NU, 128, 1024], F32, kind="ExternalInput").ap()
    gains_d = nc.dram_tensor("gains", [128, 5, 8], F32, kind="ExternalInput").ap()
    theta_d = nc.dram_tensor("theta", [128, 2, 4], F32, kind="ExternalInput").ap()
    gfin_d = nc.dram_tensor("gfin", [128, 1024], F32, kind="ExternalInput").ap()
    ident_d = nc.dram_tensor("ident", [128, 128], F32, kind="ExternalInput").ap()
    cmask_d = nc.dram_tensor("cmask", [128, 128], F32, kind="ExternalInput").ap()
    pband_d = nc.dram_tensor("pband", [128, 12, 128], F32, kind="ExternalInput").ap()
    out_d = nc.dram_tensor("out", [ntok, D], F32, kind="ExternalOutput").ap()
    wsc_d = nc.dram_tensor("wsc", [NU, 128, 1024], BF16).ap()

    S = Sched()
    with contextlib.ExitStack() as es:
        sb_addr = {}

        def sb(name, shape, d):
            t_ = es.enter_context(nc.sbuf_tensor("sb_" + name, shape, d))
            try:
                sb_addr[name] = (nc.lookup_mloc(t_).addr, int(np.prod(shape[1:])) * (4 if d == F32 else 2))
            except Exception:
                pass
            return t_

        h_t = sb("h", [128, 8 * 1024], F32)
        ring_t = sb("ring", [128, ring_r * 512], BF16)
        xn_t = sb("xn", [128, 4, 1024], BF16)
        ss_t = sb("ss", [128, 20], F32)
        lnv_t = sb("lnv", [128, 20], F32)
        rstd_t = sb("rstd", [128, 20], F32)
        xnTX_t = sb("xnTX", [128, 8, 512], BF16)
        arX_t = sb("arX", [128, 16, 512], BF16)
        arB_t = sb("arB", [128, 2048], F32)
        kT_t = sb("kT", [128, 8, 256], BF16)
        vmem_t = sb("vmem", [128, 2, 1024], BF16)
        rl_t = sb("rl", [128, 2, 512], BF16)
        xnTY_t = sb("xnTY", [128, 8, 512], BF16)
        tmpY_t = sb("tmpY", [128, 4096], F32)
        Acum_t = sb("Acum", [128, 2048], F32)
        og_t = sb("og", [128, 4, 512], BF16)
        krel_t = sb("krel", [128, 4, 512], BF16)
        qG_t = sb("qG", [128, 4, 512], BF16)
        gsil_t = sb("gsil", [128, 4, 512], BF16)
        utok_t = sb("utok", [128, 5, 512], BF16)
        vtok_t = sb("vtok", [128, 4, 512], BF16)
        ktok_t = sb("ktok", [128, 4, 512], BF16)
        sTm_t = sb("sTm", [128, 4, 512], BF16)
        P_t = sb("P", [128, 2, 4, 128], F32)
        Sbf_t = sb("Sbf", [128, 8, 4, 128], BF16)
        dec_t = sb("dec", [128, 4, 9], F32)
        osq_t = sb("osq", [128, 2, 512], BF16)
        lnb_t = sb("lnb", [128, 2, 512], F32)
        mixT_t = sb("mixT", [128, 8, 512], BF16)
        ident_t = sb("ident", [128, 128], BF16)
        mask_t = sb("mask", [128, 128], BF16)
        pband_t = sb("pband", [128, 12, 128], BF16)
        ones_t = sb("ones", [128, 128], BF16)
        cm01_t = sb("cm01", [128, 512], BF16)
        gfin_t = sb("gfin", [128, 1024], F32)
        gains_t = sb("gains", [128, 5, 8], F32)
        theta_t = sb("theta", [128, 2, 4], F32)
        lb_t = sb("lb", [128, 4], F32)
        c1_t = sb("c1", [128, 4], F32)
        nc1_t = sb("nc1", [128, 4], F32)
        banks = [es.enter_context(nc.psum_tensor("pb%d" % i, [128, 512], F32)) for i in range(8)]
        bbank = [Buf("bank%d" % i, excl=True) for i in range(8)]
        sems = {e: es.enter_context(nc.semaphore("s_" + e)) for e in ENGS}
        dsems = {q: [es.enter_context(nc.semaphore("d%s%d" % (q, i))) for i in range(n)] for q, n in NDMA_SEMS.items()}
        block = es.enter_context(nc.Block())

        bank_ctr = [0]

        def next_bank():
            i = bank_ctr[0] % 8
            bank_ctr[0] += 1
            return banks[i], bbank[i]

        def hv(buf, tb):
            return h_t[:, (buf * 4 + tb) * 1024:(buf * 4 + tb + 1) * 1024]

        def hv_all(buf):
            return h_t[:, buf * 4096:(buf + 1) * 4096].rearrange("p (tb d) -> p tb d", tb=4)

        def ringv(slot):
            return ring_t[:, slot * 512:(slot + 1) * 512]

        sigf = lambda i: tmpY_t[:, i * 512:(i + 1) * 512]
        kkv = lambda i: tmpY_t[:, 2048 + i * 512:2048 + (i + 1) * 512]
        qfv = lambda i: tmpY_t[:, 3072 + i * 512:3072 + (i + 1) * 512]
        Av = lambda hh: Acum_t[:, hh * 512:(hh + 1) * 512]
        qT_t = arX_t[:, 0:8, :]
        attnT_t = arX_t[:, 8:16, :]
        aT_all = arX_t[:, :, :].rearrange("p (g j) t -> p g j t", g=2)
        expT_all = arB_t[:, 0:1024].bitcast(BF16).rearrange("p (b m t) -> p b m t", b=2, m=2)
        rdenv = lambda i: arB_t[:, 1024 + i * 512:1024 + (i + 1) * 512]
        memfv = lambda mb: arB_t[:, mb * 1024:(mb + 1) * 1024]
        pbandf = arB_t[:, 0:1536].rearrange("p (a b) -> p a b", a=12)
        identf = arB_t[:, 1536:1664]
        maskf = arB_t[:, 1664:1792]

        b_h = [[Buf("h%d_%d" % (i, tb)) for tb in range(4)] for i in range(2)]
        b_ring = [Buf("ring%d" % i) for i in range(ring_r)]
        b_xn = [Buf("xn%d" % i) for i in range(4)]
        b_xnTX = [Buf("xnTX%d" % tb) for tb in range(4)]
        b_xnTY = [Buf("xnTY%d" % tb) for tb in range(4)]
        b_mixT = [Buf("mixT%d" % c) for c in range(8)]
        b_qT = [Buf("qT%d" % c) for c in range(8)]
        b_attnT = [Buf("attnT%d" % c) for c in range(8)]
        b_arX = b_qT + b_attnT
        b_ss = [Buf("ss%d" % i) for i in range(5)]
        b_lnv = [Buf("lnv%d" % i) for i in range(5)]
        b_rstd = [Buf("rstd%d" % i) for i in range(5)]
        b_sigf = [Buf("sigf%d" % i) for i in range(4)]
        b_kk = [Buf("kk0"), Buf("kk1")]
        b_qf = [Buf("qf0"), Buf("qf1")]
        b_cm01 = Buf("cm01")
        b_A = [Buf("A%d" % i) for i in range(4)]
        b_krel = [Buf("krel%d" % i) for i in range(4)]
        b_qG = [Buf("qG%d" % i) for i in range(4)]
        b_gsil = [Buf("gsil%d" % i) for i in range(4)]
        b_utok = [Buf("utok%d" % i) for i in range(5)]
        b_vtok = [Buf("vtok%d" % i) for i in range(4)]
        b_ktok = [Buf("ktok%d" % i) for i in range(4)]
        b_sTm = [Buf("sTm%d" % i) for i in range(4)]
        b_P = [[Buf("P%d_%d" % (i, hh)) for hh in range(4)] for i in range(2)]
        b_Sbf = [Buf("Sbf%d" % i) for i in range(8)]
        b_dec = Buf("dec")
        b_dec0 = Buf("dec0")
        b_og = [Buf("og%d" % i) for i in range(4)]
        b_osq = [Buf("osq0"), Buf("osq1")]
        b_lnb = [Buf("lnb0"), Buf("lnb1")]
        b_expT = [[Buf("expT%d_%d" % (i, m)) for m in range(2)] for i in range(2)]
        b_rden = [Buf("rden0"), Buf("rden1")]
        b_arB = b_expT[0] + b_expT[1] + b_rden
        b_kT = Buf("kT")
        b_vmem = Buf("vmem")
        b_rl = [Buf("rl0"), Buf("rl1")]
        b_ident = Buf("ident")
        b_mask = Buf("mask")
        b_pband = Buf("pband")
        b_ones = Buf("ones")
        b_gfin = Buf("gfin")
        b_gains = Buf("gains")
        b_theta = Buf("theta")
        b_lbc = Buf("lbc")
        b_wsc = [Buf("wsc%d" % u) for u in range(NU)]

        def A(eng, fn, reads=(), writes=(), dma=False):
            return S.add(eng, fn, reads=reads, writes=writes, dma=dma)

        def mm(out, lhsT, rhs, start, stop, reads, writes):
            A("pe", lambda e: e.matmul(out, lhsT=lhsT, rhs=rhs, start=start, stop=stop), reads, writes)

        def tr(out, in_, reads, writes):
            A("pe", lambda e: e.transpose(out=out, in_=in_, identity=ident_t[:]), reads + [b_ident], writes)

        def act(out, in_, func, reads, writes, scale=None, bias=None, accum_out=None):
            kw = {}
            if scale is not None:
                kw["scale"] = scale
            if bias is not None:
                kw["bias"] = bias
            if accum_out is not None:
                kw["accum_out"] = accum_out
            A("act", lambda e: e.activation(out=out, in_=in_, func=func, **kw), reads, writes)

        def ts(eng, out, in0, s1, s2, op0, op1, reads, writes):
            if s2 is None:
                A(eng, lambda e: e.tensor_scalar(out=out, in0=in0, scalar1=s1, scalar2=None, op0=op0), reads, writes)
            else:
                A(eng, lambda e: e.tensor_scalar(out=out, in0=in0, scalar1=s1, scalar2=s2, op0=op0, op1=op1),
                  reads, writes)

        def tt(eng, out, in0, in1, op, reads, writes):
            A(eng, lambda e: e.tensor_tensor(out=out, in0=in0, in1=in1, op=op), reads, writes)

        def stt(out, in0, scalar, in1, op0, op1, reads, writes):
            A("dve", lambda e: e.scalar_tensor_tensor(out=out, in0=in0, scalar=scalar, in1=in1, op0=op0, op1=op1),
              reads, writes)

        def cp(eng, out, in_, reads, writes):
            if eng == "act":
                act(out, in_, AF.Copy, reads, writes)
            else:
                A(eng, lambda e: e.tensor_copy(out=out, in_=in_), reads, writes)

        def recip(out, in_, reads, writes):
            A("dve", lambda e: e.reciprocal(out=out, in_=in_), reads, writes)

        def dma(out, in_, reads, writes):
            return A("sp", lambda e: e.dma_start(out=out, in_=in_), reads, writes, dma=True)

        dma(identf, ident_d[:, :], [], [b_arB[0]])
        dma(maskf, cmask_d[:, :], [], [b_arB[1]])
        dma(pbandf, pband_d[:, :, :], [], [b_arB[2]])
        dma(gfin_t[:], gfin_d[:, :], [], [b_gfin])
        dma(gains_t[:], gains_d[:, :, :], [], [b_gains])
        dma(theta_t[:], theta_d[:, :, :], [], [b_theta])
        cp("dve", ident_t[:], identf, [b_arB[0]], [b_ident])
        cp("dve", mask_t[:], maskf, [b_arB[1]], [b_mask])
        cp("dve", pband_t[:], pbandf, [b_arB[2]], [b_pband])
        A("dve", lambda e: e.memset(ones_t[:], 1.0), [], [b_ones])
        tt("dve", lb_t[:], theta_t[:, 0, :], theta_t[:, 1, :], ALU.subtract, [b_theta], [b_lbc])
        act(lb_t[:], lb_t[:], AF.Sigmoid, [b_lbc], [b_lbc])
        ts("dve", c1_t[:], lb_t[:], -1.0, 1.0, ALU.mult, ALU.add, [b_lbc], [b_lbc])
        ts("dve", nc1_t[:], c1_t[:], -1.0, None, ALU.mult, None, [b_lbc], [b_lbc])
        A("pool", lambda e: e.memset(cm01_t[:], 1.0), [], [b_cm01])
        A("pool", lambda e: e.memset(cm01_t[:, 0:512:64], 0.0), [], [b_cm01])

        conv_order = []
        if not record:
            seen = set()
            for u, _hf in seq_units:
                if u not in seen:
                    seen.add(u)
                    conv_order.append(u)
            assert len(conv_order) == NU
        conv_state = {"out": 0}
        conv_pos = {u: i for i, u in enumerate(conv_order)}

        def conv_upto(k_target):
            k_target = min(k_target, NU - 1)
            while conv_state["out"] <= k_target:
                u = conv_order[conv_state["out"]]
                gate = (b_h[0][0], b_h[0][1], b_h[0][2], b_h[0][3]) if conv_state["out"] == 0 else ()
                A("pool", lambda e, u=u: e.dma_start(out=wsc_d[u, :, :], in_=wall_d[u, :, :]), list(gate), [b_wsc[u]],
                  dma=True)
                conv_state["out"] += 1

        ring_state = {"loaded": 0, "cur": 0}
        CONV_AHEAD = int(_os2.environ.get("KN_CA", "10"))
        free_slots = list(range(ring_r))
        slot_of = {}

        def ring_fill():
            while free_slots and ring_state["loaded"] < len(seq_units):
                n = ring_state["loaded"]
                slot = free_slots.pop(0)
                u, hf = seq_units[n]
                conv_upto(conv_pos[u] + CONV_AHEAD)
                dma(ringv(slot), wsc_d[u, :, hf * 512:(hf + 1) * 512], [b_wsc[u]], [b_ring[slot]])
                slot_of[n] = slot
                ring_state["loaded"] += 1

        def ring_next(u, hf):
            n = ring_state["cur"]
            ring_state["cur"] += 1
            if record:
                rec_units.append((u, hf))
                return ringv(0), b_ring[0], n
            assert seq_units[n] == (u, hf), (n, seq_units[n], (u, hf))
            assert n < ring_state["loaded"], "weight ring exhausted (too many half-units held)"
            slot = slot_of[n]
            return ringv(slot), b_ring[slot], n

        def ring_done(n):
            if record:
                return
            free_slots.append(slot_of.pop(n))
            ring_fill()

        MMUS = 0.216
        import os as _os
        NORM_Y1A = float(_os.environ.get("KN_Y1A", "12"))
        NORM_Y1C = float(_os.environ.get("KN_Y1C", "12"))

        def norm_square(hbuf, grp, tb, junk, b_junk):
            act(junk, hv(hbuf, tb), AF.Square, [b_h[hbuf][tb]], b_junk + [b_ss[grp]],
                accum_out=ss_t[:, grp * 4 + tb:grp * 4 + tb + 1])

        def rms_norm_T(hbuf, grp, xnT_t, b_xnT, gi, strm, junk, b_junk, y1=7.0, y3=2.0, squares_done=False):
            sl = slice(grp * 4, grp * 4 + 4)
            if not squares_done:
                for tb in range(4):
                    norm_square(hbuf, grp, tb, junk, b_junk)
            act(lnv_t[:, sl], ss_t[:, sl], AF.Ln, [b_ss[grp]], [b_lnv[grp]], scale=1.0 / D, bias=EPS)
            act(rstd_t[:, sl], lnv_t[:, sl], AF.Exp, [b_lnv[grp]], [b_rstd[grp]], scale=-0.5)

            def scale(tb):
                si = grp * 4 + tb
                xb = strm * 2 + tb % 2
                act(xn_t[:, xb, :], hv(hbuf, tb), AF.Copy, [b_h[hbuf][tb], b_rstd[grp]], [b_xn[xb]],
                    scale=rstd_t[:, si:si + 1])

            def transp(tb):
                xb = strm * 2 + tb % 2
                bk, bb = next_bank()
                bkb = bk[:].bitcast(BF16)
                for kc in range(8):
                    tr(bkb[:, kc * 128:(kc + 1) * 128], xn_t[:, xb, kc * 128:(kc + 1) * 128], [b_xn[xb]], [bb])
                tt("dve", xnT_t[:, :, tb * 128:(tb + 1) * 128], bkb.rearrange("p (k m) -> p k m", k=8),
                   gains_t[:, gi, :].unsqueeze(2).to_broadcast([128, 8, 128]), ALU.mult, [bb, b_gains], [b_xnT[tb]])

            scale(0)
            scale(1)
            yield y1
            transp(0)
            transp(1)
            scale(2)
            scale(3)
            yield 3.5
            transp(2)
            transp(3)
            yield y3

        def proj_tokmajor(hbuf, actT, b_act, unit_base, after_tb=None):
            for half in range(2):
                us = [ring_next(unit_base + kc, half) for kc in range(8)]
                for tb in range(4):
                    bk, bb = next_bank()
                    for kc in range(8):
                        mm(bk[:], actT[:, kc, tb * 128:(tb + 1) * 128], us[kc][0], kc == 0, kc == 7,
                           [b_act[kc], us[kc][1]], [bb])
                        if tb == 3:
                            ring_done(us[kc][2])
                    hs = hv(hbuf, tb)[:, half * 512:(half + 1) * 512]
                    tt("dve", hs, bk[:], hs, ALU.add, [bb, b_h[hbuf][tb]], [b_h[hbuf][tb]])
                    if half == 1 and after_tb is not None:
                        after_tb(tb)
                    yield 8 * MMUS

        def feat_group(unit_id, xnT_t, b_xnT, n=512):
            hu = [ring_next(unit_id, 0), ring_next(unit_id, 1)]
            bk, bb = next_bank()
            for kc in range(8):
                ut, ub, un = hu[kc // 4]
                k4 = kc % 4
                mm(bk[:, 0:n], ut[:, k4 * 128:(k4 + 1) * 128], xnT_t[:, kc, 0:n], kc == 0, kc == 7, [ub] + b_xnT, [bb])
                if k4 == 3:
                    ring_done(un)
            return bk, bb

        def phaseA(g, s_i, t_i):
            hb = g % 2
            first = t_i == 0
            yield from rms_norm_T(hb, 2, xnTY_t, b_xnTY, 0, 1, osq_t[:, :, :].rearrange("p a b -> p (a b)"), b_osq, y1=NORM_Y1A, y3=4.0)
            if first:
                for i in range(2):
                    for hh in range(4):
                        A("pool", lambda e, i=i, hh=hh: e.memset(P_t[:, i, hh, :], 0.0), [], [b_P[i][hh]])
                A("pool", lambda e: e.memset(dec_t[:, :, 0:1], 1.0), [], [b_dec0])
            for hh in range(4):
                bk, bb = feat_group(U_WINF + hh, xnTY_t, b_xnTY)
                act(sigf(hh), bk[:], AF.Sigmoid, [bb], [b_sigf[hh]])
            for hh in range(4):
                i2 = hh % 2
                ts("dve", kkv(i2), sigf(hh), nc1_t[:, hh:hh + 1], c1_t[:, hh:hh + 1], ALU.mult, ALU.add,
                   [b_sigf[hh], b_lbc], [b_kk[i2]])
                act(sigf(hh), sigf(hh), AF.Ln, [b_sigf[hh], b_lbc], [b_sigf[hh]], scale=c1_t[:, hh:hh + 1],
                    bias=lb_t[:, hh:hh + 1])
                A("dve", lambda e, hh=hh: e.tensor_tensor_scan(
                    out=Av(hh), data0=cm01_t[:], data1=sigf(hh), initial=0.0, op0=ALU.mult, op1=ALU.add),
                  [b_sigf[hh], b_cm01], [b_A[hh]])
                act(dec_t[:, hh, 1:9], Av(hh)[:, 63:512:64], AF.Exp, [b_A[hh]], [b_dec])
                act(sigf(hh), Av(hh), AF.Exp, [b_A[hh]], [b_sigf[hh]], scale=-1.0)
                act(Av(hh), Av(hh), AF.Exp, [b_A[hh]], [b_A[hh]])
                tt("pool", krel_t[:, hh, :], kkv(i2), sigf(hh), ALU.mult, [b_kk[i2], b_sigf[hh]], [b_krel[hh]])
            for hh in range(4):
                bk, bb = feat_group(U_WINF + 4 + hh, xnTY_t, b_xnTY)
                i2 = hh % 2
                act(qfv(i2), bk[:], AF.Silu, [bb], [b_qf[i2]])
                tt("pool", qG_t[:, hh, :], qfv(i2), Av(hh), ALU.mult, [b_qf[i2], b_A[hh]], [b_qG[hh]])
            for hh in range(4):
                bk, bb = feat_group(U_WINF + 8 + hh, xnTY_t, b_xnTY)
                act(gsil_t[:, hh, :], bk[:], AF.Silu, [bb], [b_gsil[hh]])
            yield 32.0
            us = [ring_next(U_WINT + kc, 1) for kc in range(8)]
            for tb in range(4):
                bk, bb = next_bank()
                for kc in range(8):
                    mm(bk[:], xnTY_t[:, kc, tb * 128:(tb + 1) * 128], us[kc][0], kc == 0, kc == 7,
                       [b_xnTY[tb], us[kc][1]], [bb])
                    if tb == 3:
                        ring_done(us[kc][2])
                cp("act", vtok_t[:, tb, :], bk[:], [bb], [b_vtok[tb]])
                yield 8 * MMUS
            us = [ring_next(U_WINT + kc, 0) for kc in range(8)]
            for tb in range(4):
                bk, bb = next_bank()
                for kc in range(8):
                    mm(bk[:], xnTY_t[:, kc, tb * 128:(tb + 1) * 128], us[kc][0], kc == 0, kc == 7,
                       [b_xnTY[tb], us[kc][1]], [bb])
                    if tb == 3:
                        ring_done(us[kc][2])
                cp("dve", utok_t[:, tb + 1, :], bk[:], [bb], [b_utok[tb + 1]])
                yield 8 * MMUS
            pws = ring_next(U_POOLW, 0)

            def pool_lin(gq):
                bk2, bb2 = next_bank()
                mm(bk2[:], pws[0][:, gq * 128:(gq + 1) * 128], og_t[:, gq, :], True, True, [pws[1], b_og[gq]], [bb2])
                ts("dve", mixT_t[:, gq, :], bk2[:], gains_t[:, 4, gq:gq + 1], None, ALU.mult, None, [bb2, b_gains],
                   [b_mixT[gq]])

            for gq in range(4):
                bk, bb = next_bank()
                for tb in range(4):
                    o_ = bk[:, tb * 128:(tb + 1) * 128]
                    cur = utok_t[:, tb + 1, gq * 128:(gq + 1) * 128]
                    prv = utok_t[:, tb, gq * 128:(gq + 1) * 128]
                    if first and tb == 0:
                        mm(o_, cur, pband_t[:, 8 + gq, :], True, True, [b_utok[1], b_pband], [bb])
                    else:
                        mm(o_, cur, pband_t[:, gq, :], True, False, [b_utok[tb + 1], b_pband], [bb])
                        mm(o_, prv, pband_t[:, 4 + gq, :], False, True, [b_utok[tb], b_pband], [bb])
                cp("act", og_t[:, gq, :], bk[:], [bb], [b_og[gq]])
                if gq > 0:
                    pool_lin(gq - 1)
                yield 2.5
            pool_lin(3)
            ring_done(pws[2])
            cp("pool", utok_t[:, 0, :], utok_t[:, 4, :], [b_utok[4]], [b_utok[0]])
            for tb in range(4):
                bk, bb = next_bank()
                bkb = bk[:].bitcast(BF16)
                for hh in range(4):
                    tr(bkb[:, hh * 128:(hh + 1) * 128], krel_t[:, hh, tb * 128:(tb + 1) * 128], [b_krel[hh]], [bb])
                cp("act", ktok_t[:, tb, :], bkb[:, 0:512], [bb], [b_ktok[tb]])
                bk, bb = next_bank()
                for hh in range(4):
                    mm(bk[:, hh * 128:(hh + 1) * 128], krel_t[:, hh, tb * 128:(tb + 1) * 128],
                       qG_t[:, hh, tb * 128:(tb + 1) * 128], True, True, [b_krel[hh], b_qG[hh]], [bb])
                tt("dve", sTm_t[:, tb, :].rearrange("p (h t) -> p h t", h=4),
                   bk[:].rearrange("p (h t) -> p h t", h=4),
                   mask_t[:].unsqueeze(1).to_broadcast([128, 4, 128]), ALU.mult, [bb, b_mask], [b_sTm[tb]])
                yield 2.0
                for half in range(2):
                    c = 2 * tb + half
                    pi = c % 2
                    tt("pool", Sbf_t[:, c, :, :], P_t[:, pi, :, :],
                       dec_t[:, :, c:c + 1].to_broadcast([128, 4, 128]), ALU.mult,
                       [b_P[pi][0], b_P[pi][1], b_P[pi][2], b_P[pi][3], b_dec, b_dec0], [b_Sbf[c]])
                    bk, bb = next_bank()
                    ps = slice(half * 64, half * 64 + 64)
                    for hh in range(4):
                        mm(bk[:, hh * 128:(hh + 1) * 128], ktok_t[ps, tb, hh * 128:(hh + 1) * 128],
                           vtok_t[ps, tb, hh * 128:(hh + 1) * 128], True, True, [b_ktok[tb], b_vtok[tb]], [bb])
                    for hh in range(4):
                        stt(P_t[:, 1 - pi, hh, :], P_t[:, pi, hh, :], dec_t[:, hh, c:c + 1],
                            bk[:, hh * 128:(hh + 1) * 128], ALU.mult, ALU.add,
                            [b_P[pi][hh], b_dec, b_dec0, bb], [b_P[1 - pi][hh]])
                    yield 2.0
            cp("dve", dec_t[:, :, 0:1], dec_t[:, :, 8:9], [b_dec], [b_dec0])
            def hgrn_fin(hh):
                o2 = hh % 2
                bk2, bb2 = next_bank()
                mm(bk2[:], ones_t[:], osq_t[:, o2, :], True, True, [b_ones, b_osq[o2]], [bb2])
                act(lnb_t[:, o2, :], bk2[:], AF.Ln, [bb2], [b_lnb[o2]], scale=1.0 / 128.0, bias=EPS)
                act(lnb_t[:, o2, :], lnb_t[:, o2, :], AF.Exp, [b_lnb[o2]], [b_lnb[o2]], scale=-0.5)
                stt(mixT_t[:, 4 + hh, :], og_t[:, hh, :], gains_t[:, 4, 4 + hh:5 + hh], lnb_t[:, o2, :], ALU.mult, ALU.mult,
                    [b_og[hh], b_lnb[o2], b_gains], [b_mixT[4 + hh]])

            for hh in range(4):
                bk, bb = next_bank()
                for tb in range(4):
                    mm(bk[:, tb * 128:(tb + 1) * 128], vtok_t[:, tb, hh * 128:(hh + 1) * 128],
                       sTm_t[:, tb, hh * 128:(hh + 1) * 128], True, False, [b_vtok[tb], b_sTm[tb]], [bb])
                    for half in range(2):
                        c = 2 * tb + half
                        cs = slice(tb * 128 + half * 64, tb * 128 + half * 64 + 64)
                        mm(bk[:, cs], Sbf_t[:, c, hh, :], qG_t[:, hh, cs], False, half == 1,
                           [b_Sbf[c], b_qG[hh]], [bb])
                o2 = hh % 2
                act(osq_t[:, o2, :], bk[:], AF.Square, [bb], [b_osq[o2]])
                tt("dve", og_t[:, hh, :], bk[:], gsil_t[:, hh, :], ALU.mult, [bb, b_gsil[hh]], [b_og[hh]])
                if hh > 0:
                    hgrn_fin(hh - 1)
                yield 5.0
            hgrn_fin(3)
            yield 2.0
            junkY = osq_t[:, :, :].rearrange("p a b -> p (a b)")
            yield from proj_tokmajor(hb, mixT_t, b_mixT, U_WOUT,
                                     after_tb=lambda tb: norm_square(hb, 4, tb, junkY, b_osq))
            yield from rms_norm_T(hb, 4, xnTY_t, b_xnTY, 1, 1, junkY, b_osq, squares_done=True)

        memn_half = arB_t[:, 0:1024]
        memnT_v = arB_t[:, 1024:2048].bitcast(BF16).rearrange("p (k m) -> p k m", k=8)
        b_memf = b_expT[0] + b_expT[1]

        def kv_gen(s_i, first_block_loaded=False):
            junk = rl_t[:, :, :].rearrange("p a b -> p (a b)")
            for mb in range(2):
                if not (mb == 0 and first_block_loaded):
                    dma(memn_half, mem_d[s_i * NMEM + mb * 128:s_i * NMEM + (mb + 1) * 128, :], [], b_memf)
                act(junk, memn_half, AF.Square, b_memf, b_rl + [b_ss[1]], accum_out=ss_t[:, 4 + mb:5 + mb])
                act(lnv_t[:, 4 + mb:5 + mb], ss_t[:, 4 + mb:5 + mb], AF.Ln, [b_ss[1]], [b_lnv[1]], scale=1.0 / D, bias=EPS)
                act(rstd_t[:, 4 + mb:5 + mb], lnv_t[:, 4 + mb:5 + mb], AF.Exp, [b_lnv[1]], [b_rstd[1]], scale=-0.5)
                act(xn_t[:, mb, :], memn_half, AF.Copy, b_memf + [b_rstd[1]], [b_xn[mb]], scale=rstd_t[:, 4 + mb:5 + mb])
                yield 6.0
                bk, bb = next_bank()
                bkb = bk[:].bitcast(BF16)
                for kc in range(8):
                    tr(bkb[:, kc * 128:(kc + 1) * 128], xn_t[:, mb, kc * 128:(kc + 1) * 128], [b_xn[mb]], [bb])
                tt("dve", memnT_v[:, :, mb * 128:(mb + 1) * 128], bkb.rearrange("p (k m) -> p k m", k=8),
                   gains_t[:, 2, :].unsqueeze(2).to_broadcast([128, 8, 128]), ALU.mult, [bb, b_gains], b_rden)
                yield 2.0
            for c in range(8):
                bk, bb = feat_group(U_XK + c, memnT_v, b_rden, n=256)
                cp("act" if c % 2 == 0 else "dve", kT_t[:, c, :], bk[:, 0:256], [bb], [b_kT])
                yield 8 * 0.11
            for half in range(2):
                us = [ring_next(U_XV + kc, half) for kc in range(8)]
                for mb in range(2):
                    bk, bb = next_bank()
                    for kc in range(8):
                        mm(bk[:], memnT_v[:, kc, mb * 128:(mb + 1) * 128], us[kc][0], kc == 0, kc == 7,
                           b_rden + [us[kc][1]], [bb])
                        if mb == 1:
                            ring_done(us[kc][2])
                    cp("act" if half == 0 else "dve", vmem_t[:, mb, half * 512:(half + 1) * 512], bk[:],
                       [bb], [b_vmem])
                    yield 8 * MMUS

        def phaseX(g, s_i, t_i):
            hb = g % 2
            for c in range(8):
                bk, bb = feat_group(U_XQ + c, xnTY_t, b_xnTY)
                if c % 2 == 0:
                    act(qT_t[:, c, :], bk[:], AF.Copy, [bb], [b_qT[c]], scale=1.0 / 16.0)
                else:
                    ts("dve", qT_t[:, c, :], bk[:], 1.0 / 16.0, None, ALU.mult, None, [bb], [b_qT[c]])
                yield 8 * MMUS
            yield -1.0
            def attn_scores(a):
                e2 = a % 2
                for mc in range(2):
                    bk, bb = next_bank()
                    for j in range(2):
                        mm(bk[:], kT_t[:, 2 * a + j, mc * 128:(mc + 1) * 128], qT_t[:, 2 * a + j, :], j == 0, j == 1,
                           [b_kT, b_qT[2 * a + j]], [bb])
                    act(expT_all[:, e2, mc, :], bk[:], AF.Exp, [bb], [b_expT[e2][mc]])

            def attn_pv(a):
                e2 = a % 2
                bk, bb = next_bank()
                for mc in range(2):
                    mm(bk[:], ones_t[:], expT_all[:, e2, mc, :], mc == 0, mc == 1, [b_ones, b_expT[e2][mc]], [bb])
                act(rdenv(e2), bk[:], AF.Ln, [bb], [b_rden[e2]])
                act(rdenv(e2), rdenv(e2), AF.Exp, [b_rden[e2]], [b_rden[e2]], scale=-1.0)
                for j in range(2):
                    bk, bb = next_bank()
                    for mc in range(2):
                        mm(bk[:], vmem_t[:, mc, (2 * a + j) * 128:(2 * a + j + 1) * 128], expT_all[:, e2, mc, :],
                           mc == 0, mc == 1, [b_vmem, b_expT[e2][mc]], [bb])
                    tt("dve", attnT_t[:, 2 * a + j, :], bk[:], rdenv(e2), ALU.mult, [bb, b_rden[e2]],
                       [b_attnT[2 * a + j]])

            attn_scores(0)
            yield 3.0
            for a in range(1, 4):
                attn_scores(a)
                yield 3.0
                attn_pv(a - 1)
                yield 5.0
            attn_pv(3)
            yield 4.0
            junkX = rl_t[:, :, :].rearrange("p a b -> p (a b)")
            yield from proj_tokmajor(hb, attnT_t, b_attnT, U_XO,
                                     after_tb=lambda tb: norm_square(hb, 0, tb, junkX, b_rl))
            if t_i == ntiles - 1 and s_i + 1 < nseq:
                yield from kv_gen(s_i + 1)
            yield from rms_norm_T(hb, 0, xnTX_t, b_xnTX, 3, 0, junkX, b_rl, y1=NORM_Y1C, y3=4.0, squares_done=True)
            for gq in range(4):
                gb = gq % 2
                for jj in range(8):
                    bk, bb = feat_group(U_WUP + 8 * gq + jj, xnTX_t, b_xnTX)
                    r2 = jj % 2
                    act(rl_t[:, r2, :], bk[:], AF.Relu, [bb], [b_rl[r2]])
                    tt("pool", aT_all[:, gb, jj, :], rl_t[:, r2, :], rl_t[:, r2, :], ALU.mult, [b_rl[r2]],
                       [b_arX[8 * gb + jj]])
                    yield 8 * MMUS
                for half in range(2):
                    us = [ring_next(U_WDN + 8 * gq + jj, half) for jj in range(8)]
                    for tb in range(4):
                        bk, bb = next_bank()
                        for jj in range(8):
                            mm(bk[:], aT_all[:, gb, jj, tb * 128:(tb + 1) * 128], us[jj][0], jj == 0, jj == 7,
                               [b_arX[8 * gb + jj], us[jj][1]], [bb])
                            if tb == 3:
                                ring_done(us[jj][2])
                        hs = hv(hb, tb)[:, half * 512:(half + 1) * 512]
                        tt("dve", hs, bk[:], hs, ALU.add, [bb, b_h[hb][tb]], [b_h[hb][tb]])
                        if gq == 3 and half == 1:
                            norm_square(hb, 3, tb, junkX, b_rl)
                        yield 8 * MMUS
            act(lnv_t[:, 12:16], ss_t[:, 12:16], AF.Ln, [b_ss[3]], [b_lnv[3]], scale=1.0 / D, bias=EPS)
            act(rstd_t[:, 12:16], lnv_t[:, 12:16], AF.Exp, [b_lnv[3]], [b_rstd[3]], scale=-0.5)
            for tb in range(4):
                stt(hv(hb, tb), hv(hb, tb), rstd_t[:, 12 + tb:13 + tb], gfin_t[:], ALU.mult, ALU.mult,
                    [b_h[hb][tb], b_rstd[3], b_gfin], [b_h[hb][tb]])
                if tb % 2 == 1:
                    hf = tb // 2
                    st = dma(out_d[g * T + hf * 256:g * T + (hf + 1) * 256, :].rearrange("(tb p) d -> p tb d", p=128),
                             hv_all(hb)[:, 2 * hf:2 * hf + 2, :], b_h[hb][2 * hf:2 * hf + 2], [])
                    out_stores.append(st)
            yield 8.0

        out_stores = []
        ntot = nseq * ntiles

        def load_x(g):
            buf = g % 2
            for hf in range(2):
                dma(hv_all(buf)[:, 2 * hf:2 * hf + 2, :],
                    x_d[g * T + hf * 256:g * T + (hf + 1) * 256, :].rearrange("(tb p) d -> p tb d", p=128),
                    [], b_h[buf][2 * hf:2 * hf + 2])

        def drive(gx, gy, head_start=0.0, gate=True, x_delay=0.0):
            tx = x_delay
            ty = None if (gx is not None and gate) else 0.0
            while gx is not None or gy is not None:
                if gx is not None and (gy is None or ty is None or tx <= ty):
                    try:
                        c = next(gx)
                        if c < 0:
                            if ty is None:
                                ty = tx + head_start
                        else:
                            tx += c
                    except StopIteration:
                        gx = None
                        if ty is None:
                            ty = tx
                else:
                    try:
                        ty += next(gy)
                    except StopIteration:
                        gy = None

        def coords(g):
            return g, g // ntiles, g % ntiles

        load_x(0)
        if not record:
            ring_fill()
        if ntot > 1:
            load_x(1)
        drive(kv_gen(0), phaseA(*coords(0)), gate=False, x_delay=70.0)
        for g in range(ntot):
            gy = phaseA(*coords(g + 1)) if g + 1 < ntot else None
            if interleave:
                drive(phaseX(*coords(g)), gy, head_start=x_head)
            else:
                drive(phaseX(*coords(g)), None)
                drive(None, gy)
            if g + 2 < ntot:
                load_x(g + 2)
        if not record:
            conv_upto(NU - 1)

        if record:
            return rec_units
        _DEBUG_INFO["sb_addr"] = dict(sb_addr)
        S.emit(block, sems, dsems, final_wait_ops=out_stores)
    return nc


_CACHE = {}


def kernel(**inputs):
    x = np.asarray(inputs["x"], np.float32)
    mem = np.asarray(inputs["mem"], np.float32)
    B = x.shape[0]
    seq = x.shape[1]
    ntiles = seq // T
    nseq = B // NCORES
    wall, gains, theta, gfin = _host_layout(inputs)
    ident, cmask, pband = _host_consts()
    key = (ntiles, nseq)
    if key not in _CACHE:
        _CACHE[key] = build_program(ntiles=ntiles, nseq=nseq)
    nc = _CACHE[key]
    in_maps = []
    for c in range(NCORES):
        in_maps.append({
            "x": np.ascontiguousarray(x[c * nseq:(c + 1) * nseq].reshape(nseq * seq, D)),
            "mem": np.ascontiguousarray(mem[c * nseq:(c + 1) * nseq].reshape(nseq * NMEM, D)),
            "wall": wall, "gains": gains, "theta": theta, "gfin": gfin,
            "ident": ident, "cmask": cmask, "pband": pband,
        })
    res = run_bass_kernel_spmd(nc, in_maps, core_ids=list(range(NCORES)))
    outs = [np.asarray(r["out"]).reshape(nseq, seq, D) for r in res.results]
    return np.concatenate(outs, axis=0).astype(np.float32)
```

```python
import contextlib
import os as _os2

import numpy as np
import concourse.bass as bass
import concourse.mybir as mybir
from concourse.bass_utils import run_bass_kernel_spmd

dt = mybir.dt
F32 = dt.float32
BF16 = dt.bfloat16
AF = mybir.ActivationFunctionType
ALU = mybir.AluOpType

NCORES = 8
D = 1024
SEQ = 4096
T = 512
NSEQ = 2
NMEM = 256
EPS = 1e-6

U_WINF = 0
U_WINT = 12
U_POOLW = 20
U_WOUT = 21
U_XQ = 29
U_XO = 37
U_WUP = 45
U_WDN = 77
U_XK = 109
U_XV = 117
NU = 125


def _fm(w, c):
    blk = w[:, c * 128:(c + 1) * 128]
    return blk.reshape(8, 128, 128).transpose(1, 0, 2).reshape(128, 1024)


def _host_layout(inp):
    w_in = np.asarray(inp["w_in"], np.float32)[0]
    wall = np.zeros((NU, 128, 1024), np.float32)
    for hh in range(4):
        wall[U_WINF + hh] = _fm(w_in, 8 + hh)
        wall[U_WINF + 4 + hh] = _fm(w_in, 4 + hh)
        wall[U_WINF + 8 + hh] = _fm(w_in, 16 + hh)
    for kc in range(8):
        wall[U_WINT + kc, :, 0:512] = w_in[kc * 128:(kc + 1) * 128, 0:512]
        wall[U_WINT + kc, :, 512:1024] = w_in[kc * 128:(kc + 1) * 128, 1536:2048]
    pw = np.asarray(inp["pool_w"], np.float32)[0]
    for g in range(4):
        wall[U_POOLW, :, g * 128:(g + 1) * 128] = pw[g]
    w_out = np.asarray(inp["w_out"], np.float32)[0]
    xw_q = np.asarray(inp["xw_q"], np.float32)[0]
    xw_o = np.asarray(inp["xw_o"], np.float32)[0]
    xw_kv = np.asarray(inp["xw_kv"], np.float32)[0]
    w_up = np.asarray(inp["w_up"], np.float32)[0]
    w_dn = np.asarray(inp["w_down"], np.float32)[0]
    for kc in range(8):
        wall[U_WOUT + kc] = w_out[kc * 128:(kc + 1) * 128]
        wall[U_XO + kc] = xw_o[kc * 128:(kc + 1) * 128]
        wall[U_XQ + kc] = _fm(xw_q, kc)
        wall[U_XK + kc] = _fm(xw_kv, kc)
        wall[U_XV + kc] = xw_kv[kc * 128:(kc + 1) * 128, 1024:2048]
    for j in range(32):
        wall[U_WUP + j] = _fm(w_up, j)
        wall[U_WDN + j] = w_dn[j * 128:(j + 1) * 128]

    def pk(v):
        return np.asarray(v, np.float32).reshape(8, 128).T

    gains = np.zeros((128, 5, 8), np.float32)
    gains[:, 0] = pk(inp["norm_mix"][0])
    gains[:, 1] = pk(inp["norm_xq"][0])
    gains[:, 2] = pk(inp["norm_mem"][0])
    gains[:, 3] = pk(inp["norm_mlp"][0])
    gains[:, 4] = pk(np.concatenate([np.asarray(inp["pool_scale"], np.float32)[0],
                                     np.asarray(inp["hgrn_norm"], np.float32)[0]]))
    th = np.asarray(inp["lb_theta"], np.float32)
    theta = np.ascontiguousarray(th.reshape(2, 4, 128).transpose(2, 0, 1))
    gfin = np.ascontiguousarray(np.broadcast_to(np.asarray(inp["norm_final"], np.float32), (128, 1024)))
    return wall, np.ascontiguousarray(gains), theta, gfin


def _host_consts():
    ident = np.eye(128, dtype=np.float32)
    s = np.arange(128)[:, None]
    t = np.arange(128)[None, :]
    cmask = ((s // 64 == t // 64) & (s <= t)).astype(np.float32)
    pband = np.zeros((128, 12, 128), np.float32)
    for g, w in enumerate((2, 4, 8, 16)):
        inwin = ((t - s) >= 0) & ((t - s) < w)
        pband[:, g, :] = inwin / w - (s == t)
        d = t + 128 - s
        pband[:, 4 + g, :] = ((d >= 0) & (d < w)) / w
        cnt = np.minimum(t + 1, w)
        pband[:, 8 + g, :] = inwin / cnt - (s == t)
    return ident, cmask, pband


class Buf:
    __slots__ = ("name", "w", "r", "excl")

    def __init__(self, name="", excl=False):
        self.name = name
        self.w = None
        self.r = []
        self.excl = excl


class Op:
    __slots__ = ("eng", "fn", "deps", "isdma", "sem", "val", "needed", "idx")


ENGS = ("pe", "act", "dve", "pool", "sp")
NDMA_SEMS = {"sp": 12, "pool": 8}


class Sched:
    def __init__(self):
        self.q = {e: [] for e in ENGS}
        self.dma_ops = {e: [] for e in NDMA_SEMS}

    def add(self, eng, fn, reads=(), writes=(), dma=False):
        op = Op()
        op.eng = eng
        op.fn = fn
        op.isdma = dma
        op.needed = False
        op.sem = None
        op.val = 0
        if any(b.excl for b in reads):
            writes = list(writes) + [b for b in reads if b.excl]
            reads = [b for b in reads if not b.excl]
        keep = {}
        for b in reads:
            d = b.w
            if d is not None:
                if d.isdma or dma or d.eng != eng or eng != "pe":
                    keep[id(d)] = d
        same_ok = eng == "pe"
        for b in writes:
            d = b.w
            if d is not None and (d.isdma or dma or d.eng != eng or not same_ok):
                keep[id(d)] = d
            for d in b.r:
                if d.isdma or dma or d.eng != eng or not same_ok:
                    keep[id(d)] = d
        op.deps = list(keep.values())
        for b in reads:
            b.r.append(op)
        for b in writes:
            b.w = op
            b.r = []
        if dma:
            op.idx = len(self.dma_ops[eng])
            self.dma_ops[eng].append(op)
        self.q[eng].append(op)
        return op

    def emit(self, block, sems, dma_sems, final_wait_ops=()):
        for e in ENGS:
            for op in self.q[e]:
                for d in op.deps:
                    d.needed = True
        for op in final_wait_ops:
            op.needed = True
        for e in ENGS:
            cnt = 0
            for op in self.q[e]:
                if op.isdma:
                    op.sem = dma_sems[e][op.idx % NDMA_SEMS[e]]
                    op.val = 16 * (op.idx // NDMA_SEMS[e] + 1)
                elif op.needed:
                    cnt += 1
                    op.sem = sems[e]
                    op.val = cnt
        dma_ops = self.dma_ops

        def run(e, eng):
            waited = {}
            for op in self.q[e]:
                waits = {}
                for d in op.deps:
                    key = id(d.sem)
                    if key not in waits or waits[key][1] < d.val:
                        waits[key] = (d.sem, d.val)
                if op.isdma and op.idx >= NDMA_SEMS[e]:
                    prev = dma_ops[e][op.idx - NDMA_SEMS[e]]
                    key = id(prev.sem)
                    if key not in waits or waits[key][1] < prev.val:
                        waits[key] = (prev.sem, prev.val)
                for key, (s, v) in waits.items():
                    if waited.get(key, 0) >= v:
                        continue
                    waited[key] = v
                    eng.wait_ge(s, v)
                ins = op.fn(eng)
                if op.isdma:
                    ins.then_inc(op.sem, 16)
                elif op.needed:
                    ins.then_inc(op.sem, 1)
            if e == "sp":
                for op in final_wait_ops:
                    if waited.get(id(op.sem), 0) < op.val:
                        waited[id(op.sem)] = op.val
                        eng.wait_ge(op.sem, op.val)

        @block.tensor
        def _(eng):
            run("pe", eng)

        @block.scalar
        def _(eng):
            run("act", eng)

        @block.vector
        def _(eng):
            run("dve", eng)

        @block.gpsimd
        def _(eng):
            run("pool", eng)

        @block.sync
        def _(eng):
            run("sp", eng)


_DEBUG_INFO = {}


def _tile_units(first):
    lst = list(range(U_WINF, U_WINF + 12)) + list(range(U_WINT, U_WINT + 8)) + [U_POOLW]
    lst += list(range(U_WOUT, U_WOUT + 8))
    return lst


def build_program(ntiles=SEQ // T, nseq=NSEQ, ring_r=24, interleave=True, x_head=8.0):
    seq = _build(ntiles, nseq, ring_r, interleave, None, x_head)
    return _build(ntiles, nseq, ring_r, interleave, seq, x_head)


def _build(ntiles, nseq, ring_r, interleave, seq_units, x_head=8.0):
    record = seq_units is None
    rec_units = []
    nc = bass.Bass("TRN2", target_bir_lowering=False)
    ntok = nseq * ntiles * T
    x_d = nc.dram_tensor("x", [ntok, D], F32, kind="ExternalInput").ap()
    mem_d = nc.dram_tensor("mem", [nseq * NMEM, D], F32, kind="ExternalInput").ap()
    wall_d = nc.dram_tensor("wall", [NU, 128, 1024], F32, kind="ExternalInput").ap()
    gains_d = nc.dram_tensor("gains", [128, 5, 8], F32, kind="ExternalInput").ap()
    theta_d = nc.dram_tensor("theta", [128, 2, 4], F32, kind="ExternalInput").ap()
    gfin_d = nc.dram_tensor("gfin", [128, 1024], F32, kind="ExternalInput").ap()
    ident_d = nc.dram_tensor("ident", [128, 128], F32, kind="ExternalInput").ap()
    cmask_d = nc.dram_tensor("cmask", [128, 128], F32, kind="ExternalInput").ap()
    pband_d = nc.dram_tensor("pband", [128, 12, 128], F32, kind="ExternalInput").ap()
    out_d = nc.dram_tensor("out", [ntok, D], F32, kind="ExternalOutput").ap()
    wsc_d = nc.dram_tensor("wsc", [NU, 128, 1024], BF16).ap()

    S = Sched()
    with contextlib.ExitStack() as es:
        sb_addr = {}

        def sb(name, shape, d):
            t_ = es.enter_context(nc.sbuf_tensor("sb_" + name, shape, d))
            try:
                sb_addr[name] = (nc.lookup_mloc(t_).addr, int(np.prod(shape[1:])) * (4 if d == F32 else 2))
            except Exception:
                pass
            return t_

        h_t = sb("h", [128, 8 * 1024], F32)
        ring_t = sb("ring", [128, ring_r * 512], BF16)
        xn_t = sb("xn", [128, 4, 1024], BF16)
        ss_t = sb("ss", [128, 20], F32)
        lnv_t = sb("lnv", [128, 20], F32)
        rstd_t = sb("rstd", [128, 20], F32)
        xnTX_t = sb("xnTX", [128, 8, 512], BF16)
        arX_t = sb("arX", [128, 16, 512], BF16)
        arB_t = sb("arB", [128, 2048], F32)
        kT_t = sb("kT", [128, 8, 256], BF16)
        vmem_t = sb("vmem", [128, 2, 1024], BF16)
        rl_t = sb("rl", [128, 2, 512], BF16)
        xnTY_t = sb("xnTY", [128, 8, 512], BF16)
        tmpY_t = sb("tmpY", [128, 4096], F32)
        Acum_t = sb("Acum", [128, 2048], F32)
        og_t = sb("og", [128, 4, 512], BF16)
        krel_t = sb("krel", [128, 4, 512], BF16)
        qG_t = sb("qG", [128, 4, 512], BF16)
        gsil_t = sb("gsil", [128, 4, 512], BF16)
        utok_t = sb("utok", [128, 5, 512], BF16)
        vtok_t = sb("vtok", [128, 4, 512], BF16)
        ktok_t = sb("ktok", [128, 4, 512], BF16)
        sTm_t = sb("sTm", [128, 4, 512], BF16)
        P_t = sb("P", [128, 2, 4, 128], F32)
        Sbf_t = sb("Sbf", [128, 8, 4, 128], BF16)
        dec_t = sb("dec", [128, 4, 9], F32)
        osq_t = sb("osq", [128, 2, 512], BF16)
        lnb_t = sb("lnb", [128, 2, 512], F32)
        mixT_t = sb("mixT", [128, 8, 512], BF16)
        ident_t = sb("ident", [128, 128], BF16)
        mask_t = sb("mask", [128, 128], BF16)
        pband_t = sb("pband", [128, 12, 128], BF16)
        ones_t = sb("ones", [128, 128], BF16)
        cm01_t = sb("cm01", [128, 512], BF16)
        gfin_t = sb("gfin", [128, 1024], F32)
        gains_t = sb("gains", [128, 5, 8], F32)
        theta_t = sb("theta", [128, 2, 4], F32)
        lb_t = sb("lb", [128, 4], F32)
        c1_t = sb("c1", [128, 4], F32)
        nc1_t = sb("nc1", [128, 4], F32)
        banks = [es.enter_context(nc.psum_tensor("pb%d" % i, [128, 512], F32)) for i in range(8)]
        bbank = [Buf("bank%d" % i, excl=True) for i in range(8)]
        sems = {e: es.enter_context(nc.semaphore("s_" + e)) for e in ENGS}
        dsems = {q: [es.enter_context(nc.semaphore("d%s%d" % (q, i))) for i in range(n)] for q, n in NDMA_SEMS.items()}
        block = es.enter_context(nc.Block())

        bank_ctr = [0]

        def next_bank():
            i = bank_ctr[0] % 8
            bank_ctr[0] += 1
            return banks[i], bbank[i]

        def hv(buf, tb):
            return h_t[:, (buf * 4 + tb) * 1024:(buf * 4 + tb + 1) * 1024]

        def hv_all(buf):
            return h_t[:, buf * 4096:(buf + 1) * 4096].rearrange("p (tb d) -> p tb d", tb=4)

        def ringv(slot):
            return ring_t[:, slot * 512:(slot + 1) * 512]

        sigf = lambda i: tmpY_t[:, i * 512:(i + 1) * 512]
        kkv = lambda i: tmpY_t[:, 2048 + i * 512:2048 + (i + 1) * 512]
        qfv = lambda i: tmpY_t[:, 3072 + i * 512:3072 + (i + 1) * 512]
        Av = lambda hh: Acum_t[:, hh * 512:(hh + 1) * 512]
        qT_t = arX_t[:, 0:8, :]
        attnT_t = arX_t[:, 8:16, :]
        aT_all = arX_t[:, :, :].rearrange("p (g j) t -> p g j t", g=2)
        expT_all = arB_t[:, 0:1024].bitcast(BF16).rearrange("p (b m t) -> p b m t", b=2, m=2)
        rdenv = lambda i: arB_t[:, 1024 + i * 512:1024 + (i + 1) * 512]
        memfv = lambda mb: arB_t[:, mb * 1024:(mb + 1) * 1024]
        pbandf = arB_t[:, 0:1536].rearrange("p (a b) -> p a b", a=12)
        identf = arB_t[:, 1536:1664]
        maskf = arB_t[:, 1664:1792]

        b_h = [[Buf("h%d_%d" % (i, tb)) for tb in range(4)] for i in range(2)]
        b_ring = [Buf("ring%d" % i) for i in range(ring_r)]
        b_xn = [Buf("xn%d" % i) for i in range(4)]
        b_xnTX = [Buf("xnTX%d" % tb) for tb in range(4)]
        b_xnTY = [Buf("xnTY%d" % tb) for tb in range(4)]
        b_mixT = [Buf("mixT%d" % c) for c in range(8)]
        b_qT = [Buf("qT%d" % c) for c in range(8)]
        b_attnT = [Buf("attnT%d" % c) for c in range(8)]
        b_arX = b_qT + b_attnT
        b_ss = [Buf("ss%d" % i) for i in range(5)]
        b_lnv = [Buf("lnv%d" % i) for i in range(5)]
        b_rstd = [Buf("rstd%d" % i) for i in range(5)]
        b_sigf = [Buf("sigf%d" % i) for i in range(4)]
        b_kk = [Buf("kk0"), Buf("kk1")]
        b_qf = [Buf("qf0"), Buf("qf1")]
        b_cm01 = Buf("cm01")
        b_A = [Buf("A%d" % i) for i in range(4)]
        b_krel = [Buf("krel%d" % i) for i in range(4)]
        b_qG = [Buf("qG%d" % i) for i in range(4)]
        b_gsil = [Buf("gsil%d" % i) for i in range(4)]
        b_utok = [Buf("utok%d" % i) for i in range(5)]
        b_vtok = [Buf("vtok%d" % i) for i in range(4)]
        b_ktok = [Buf("ktok%d" % i) for i in range(4)]
        b_sTm = [Buf("sTm%d" % i) for i in range(4)]
        b_P = [[Buf("P%d_%d" % (i, hh)) for hh in range(4)] for i in range(2)]
        b_Sbf = [Buf("Sbf%d" % i) for i in range(8)]
        b_dec = Buf("dec")
        b_dec0 = Buf("dec0")
        b_og = [Buf("og%d" % i) for i in range(4)]
        b_osq = [Buf("osq0"), Buf("osq1")]
        b_lnb = [Buf("lnb0"), Buf("lnb1")]
        b_expT = [[Buf("expT%d_%d" % (i, m)) for m in range(2)] for i in range(2)]
        b_rden = [Buf("rden0"), Buf("rden1")]
        b_arB = b_expT[0] + b_expT[1] + b_rden
        b_kT = Buf("kT")
        b_vmem = Buf("vmem")
        b_rl = [Buf("rl0"), Buf("rl1")]
        b_ident = Buf("ident")
        b_mask = Buf("mask")
        b_pband = Buf("pband")
        b_ones = Buf("ones")
        b_gfin = Buf("gfin")
        b_gains = Buf("gains")
        b_theta = Buf("theta")
        b_lbc = Buf("lbc")
        b_wsc = [Buf("wsc%d" % u) for u in range(NU)]

        def A(eng, fn, reads=(), writes=(), dma=False):
            return S.add(eng, fn, reads=reads, writes=writes, dma=dma)

        def mm(out, lhsT, rhs, start, stop, reads, writes):
            A("pe", lambda e: e.matmul(out, lhsT=lhsT, rhs=rhs, start=start, stop=stop), reads, writes)

        def tr(out, in_, reads, writes):
            A("pe", lambda e: e.transpose(out=out, in_=in_, identity=ident_t[:]), reads + [b_ident], writes)

        def act(out, in_, func, reads, writes, scale=None, bias=None, accum_out=None):
            kw = {}
            if scale is not None:
                kw["scale"] = scale
            if bias is not None:
                kw["bias"] = bias
            if accum_out is not None:
                kw["accum_out"] = accum_out
            A("act", lambda e: e.activation(out=out, in_=in_, func=func, **kw), reads, writes)

        def ts(eng, out, in0, s1, s2, op0, op1, reads, writes):
            if s2 is None:
                A(eng, lambda e: e.tensor_scalar(out=out, in0=in0, scalar1=s1, scalar2=None, op0=op0), reads, writes)
            else:
                A(eng, lambda e: e.tensor_scalar(out=out, in0=in0, scalar1=s1, scalar2=s2, op0=op0, op1=op1),
                  reads, writes)

        def tt(eng, out, in0, in1, op, reads, writes):
            A(eng, lambda e: e.tensor_tensor(out=out, in0=in0, in1=in1, op=op), reads, writes)

        def stt(out, in0, scalar, in1, op0, op1, reads, writes):
            A("dve", lambda e: e.scalar_tensor_tensor(out=out, in0=in0, scalar=scalar, in1=in1, op0=op0, op1=op1),
              reads, writes)

        def cp(eng, out, in_, reads, writes):
            if eng == "act":
                act(out, in_, AF.Copy, reads, writes)
            else:
                A(eng, lambda e: e.tensor_copy(out=out, in_=in_), reads, writes)

        def recip(out, in_, reads, writes):
            A("dve", lambda e: e.reciprocal(out=out, in_=in_), reads, writes)

        def dma(out, in_, reads, writes):
            return A("sp", lambda e: e.dma_start(out=out, in_=in_), reads, writes, dma=True)

        dma(identf, ident_d[:, :], [], [b_arB[0]])
        dma(maskf, cmask_d[:, :], [], [b_arB[1]])
        dma(pbandf, pband_d[:, :, :], [], [b_arB[2]])
        dma(gfin_t[:], gfin_d[:, :], [], [b_gfin])
        dma(gains_t[:], gains_d[:, :, :], [], [b_gains])
        dma(theta_t[:], theta_d[:, :, :], [], [b_theta])
        cp("dve", ident_t[:], identf, [b_arB[0]], [b_ident])
        cp("dve", mask_t[:], maskf, [b_arB[1]], [b_mask])
        cp("dve", pband_t[:], pbandf, [b_arB[2]], [b_pband])
        A("dve", lambda e: e.memset(ones_t[:], 1.0), [], [b_ones])
        tt("dve", lb_t[:], theta_t[:, 0, :], theta_t[:, 1, :], ALU.subtract, [b_theta], [b_lbc])
        act(lb_t[:], lb_t[:], AF.Sigmoid, [b_lbc], [b_lbc])
        ts("dve", c1_t[:], lb_t[:], -1.0, 1.0, ALU.mult, ALU.add, [b_lbc], [b_lbc])
        ts("dve", nc1_t[:], c1_t[:], -1.0, None, ALU.mult, None, [b_lbc], [b_lbc])
        A("pool", lambda e: e.memset(cm01_t[:], 1.0), [], [b_cm01])
        A("pool", lambda e: e.memset(cm01_t[:, 0:512:64], 0.0), [], [b_cm01])

        conv_order = []
        if not record:
            seen = set()
            for u, _hf in seq_units:
                if u not in seen:
                    seen.add(u)
                    conv_order.append(u)
            assert len(conv_order) == NU
        conv_state = {"out": 0}
        conv_pos = {u: i for i, u in enumerate(conv_order)}

        def conv_upto(k_target):
            k_target = min(k_target, NU - 1)
            while conv_state["out"] <= k_target:
                u = conv_order[conv_state["out"]]
                gate = (b_h[0][0], b_h[0][1], b_h[0][2], b_h[0][3]) if conv_state["out"] == 0 else ()
                A("pool", lambda e, u=u: e.dma_start(out=wsc_d[u, :, :], in_=wall_d[u, :, :]), list(gate), [b_wsc[u]],
                  dma=True)
                conv_state["out"] += 1

        ring_state = {"loaded": 0, "cur": 0}
        CONV_AHEAD = int(_os2.environ.get("KN_CA", "10"))
        free_slots = list(range(ring_r))
        slot_of = {}

        def ring_fill():
            while free_slots and ring_state["loaded"] < len(seq_units):
                n = ring_state["loaded"]
                slot = free_slots.pop(0)
                u, hf = seq_units[n]
                conv_upto(conv_pos[u] + CONV_AHEAD)
                dma(ringv(slot), wsc_d[u, :, hf * 512:(hf + 1) * 512], [b_wsc[u]], [b_ring[slot]])
                slot_of[n] = slot
                ring_state["loaded"] += 1

        def ring_next(u, hf):
            n = ring_state["cur"]
            ring_state["cur"] += 1
            if record:
                rec_units.append((u, hf))
                return ringv(0), b_ring[0], n
            assert seq_units[n] == (u, hf), (n, seq_units[n], (u, hf))
            assert n < ring_state["loaded"], "weight ring exhausted (too many half-units held)"
            slot = slot_of[n]
            return ringv(slot), b_ring[slot], n

        def ring_done(n):
            if record:
                return
            free_slots.append(slot_of.pop(n))
            ring_fill()

        MMUS = 0.216
        import os as _os
        NORM_Y1A = float(_os.environ.get("KN_Y1A", "18"))
        NORM_Y1C = float(_os.environ.get("KN_Y1C", "12"))

        def norm_square(hbuf, grp, tb, junk, b_junk):
            act(junk, hv(hbuf, tb), AF.Square, [b_h[hbuf][tb]], b_junk + [b_ss[grp]],
                accum_out=ss_t[:, grp * 4 + tb:grp * 4 + tb + 1])

        def rms_norm_T(hbuf, grp, xnT_t, b_xnT, gi, strm, junk, b_junk, y1=7.0, y3=2.0, squares_done=False):
            sl = slice(grp * 4, grp * 4 + 4)
            if not squares_done:
                for tb in range(4):
                    norm_square(hbuf, grp, tb, junk, b_junk)
            act(lnv_t[:, sl], ss_t[:, sl], AF.Ln, [b_ss[grp]], [b_lnv[grp]], scale=1.0 / D, bias=EPS)
            act(rstd_t[:, sl], lnv_t[:, sl], AF.Exp, [b_lnv[grp]], [b_rstd[grp]], scale=-0.5)

            def scale(tb):
                si = grp * 4 + tb
                xb = strm * 2 + tb % 2
                act(xn_t[:, xb, :], hv(hbuf, tb), AF.Copy, [b_h[hbuf][tb], b_rstd[grp]], [b_xn[xb]],
                    scale=rstd_t[:, si:si + 1])

            def transp(tb):
                xb = strm * 2 + tb % 2
                bk, bb = next_bank()
                bkb = bk[:].bitcast(BF16)
                for kc in range(8):
                    tr(bkb[:, kc * 128:(kc + 1) * 128], xn_t[:, xb, kc * 128:(kc + 1) * 128], [b_xn[xb]], [bb])
                tt("dve", xnT_t[:, :, tb * 128:(tb + 1) * 128], bkb.rearrange("p (k m) -> p k m", k=8),
                   gains_t[:, gi, :].unsqueeze(2).to_broadcast([128, 8, 128]), ALU.mult, [bb, b_gains], [b_xnT[tb]])

            scale(0)
            scale(1)
            yield y1
            transp(0)
            transp(1)
            scale(2)
            scale(3)
            yield 3.5
            transp(2)
            transp(3)
            yield y3

        def proj_tokmajor(hbuf, actT, b_act, unit_base, after_tb=None):
            for half in range(2):
                us = [ring_next(unit_base + kc, half) for kc in range(8)]
                for tb in range(4):
                    bk, bb = next_bank()
                    for kc in range(8):
                        mm(bk[:], actT[:, kc, tb * 128:(tb + 1) * 128], us[kc][0], kc == 0, kc == 7,
                           [b_act[kc], us[kc][1]], [bb])
                        if tb == 3:
                            ring_done(us[kc][2])
                    hs = hv(hbuf, tb)[:, half * 512:(half + 1) * 512]
                    tt("dve", hs, bk[:], hs, ALU.add, [bb, b_h[hbuf][tb]], [b_h[hbuf][tb]])
                    if half == 1 and after_tb is not None:
                        after_tb(tb)
                    yield 8 * MMUS

        def feat_group(unit_id, xnT_t, b_xnT, n=512):
            hu = [ring_next(unit_id, 0), ring_next(unit_id, 1)]
            bk, bb = next_bank()
            for kc in range(8):
                ut, ub, un = hu[kc // 4]
                k4 = kc % 4
                mm(bk[:, 0:n], ut[:, k4 * 128:(k4 + 1) * 128], xnT_t[:, kc, 0:n], kc == 0, kc == 7, [ub] + b_xnT, [bb])
                if k4 == 3:
                    ring_done(un)
            return bk, bb

        def phaseA(g, s_i, t_i):
            hb = g % 2
            first = t_i == 0
            yield from rms_norm_T(hb, 2, xnTY_t, b_xnTY, 0, 1, osq_t[:, :, :].rearrange("p a b -> p (a b)"), b_osq, y1=NORM_Y1A, y3=4.0)
            if first:
                for i in range(2):
                    for hh in range(4):
                        A("pool", lambda e, i=i, hh=hh: e.memset(P_t[:, i, hh, :], 0.0), [], [b_P[i][hh]])
                A("pool", lambda e: e.memset(dec_t[:, :, 0:1], 1.0), [], [b_dec0])
            for hh in range(4):
                bk, bb = feat_group(U_WINF + hh, xnTY_t, b_xnTY)
                act(sigf(hh), bk[:], AF.Sigmoid, [bb], [b_sigf[hh]])
            for hh in range(4):
                i2 = hh % 2
                ts("dve", kkv(i2), sigf(hh), nc1_t[:, hh:hh + 1], c1_t[:, hh:hh + 1], ALU.mult, ALU.add,
                   [b_sigf[hh], b_lbc], [b_kk[i2]])
                act(sigf(hh), sigf(hh), AF.Ln, [b_sigf[hh], b_lbc], [b_sigf[hh]], scale=c1_t[:, hh:hh + 1],
                    bias=lb_t[:, hh:hh + 1])
                A("dve", lambda e, hh=hh: e.tensor_tensor_scan(
                    out=Av(hh), data0=cm01_t[:], data1=sigf(hh), initial=0.0, op0=ALU.mult, op1=ALU.add),
                  [b_sigf[hh], b_cm01], [b_A[hh]])
                act(dec_t[:, hh, 1:9], Av(hh)[:, 63:512:64], AF.Exp, [b_A[hh]], [b_dec])
                act(sigf(hh), Av(hh), AF.Exp, [b_A[hh]], [b_sigf[hh]], scale=-1.0)
                act(Av(hh), Av(hh), AF.Exp, [b_A[hh]], [b_A[hh]])
                tt("pool", krel_t[:, hh, :], kkv(i2), sigf(hh), ALU.mult, [b_kk[i2], b_sigf[hh]], [b_krel[hh]])
            for hh in range(4):
                bk, bb = feat_group(U_WINF + 4 + hh, xnTY_t, b_xnTY)
                i2 = hh % 2
                act(qfv(i2), bk[:], AF.Silu, [bb], [b_qf[i2]])
                tt("pool", qG_t[:, hh, :], qfv(i2), Av(hh), ALU.mult, [b_qf[i2], b_A[hh]], [b_qG[hh]])
            for hh in range(4):
                bk, bb = feat_group(U_WINF + 8 + hh, xnTY_t, b_xnTY)
                act(gsil_t[:, hh, :], bk[:], AF.Silu, [bb], [b_gsil[hh]])
            yield 32.0
            us = [ring_next(U_WINT + kc, 1) for kc in range(8)]
            for tb in range(4):
                bk, bb = next_bank()
                for kc in range(8):
                    mm(bk[:], xnTY_t[:, kc, tb * 128:(tb + 1) * 128], us[kc][0], kc == 0, kc == 7,
                       [b_xnTY[tb], us[kc][1]], [bb])
                    if tb == 3:
                        ring_done(us[kc][2])
                cp("act", vtok_t[:, tb, :], bk[:], [bb], [b_vtok[tb]])
                yield 8 * MMUS
            us = [ring_next(U_WINT + kc, 0) for kc in range(8)]
            for tb in range(4):
                bk, bb = next_bank()
                for kc in range(8):
                    mm(bk[:], xnTY_t[:, kc, tb * 128:(tb + 1) * 128], us[kc][0], kc == 0, kc == 7,
                       [b_xnTY[tb], us[kc][1]], [bb])
                    if tb == 3:
                        ring_done(us[kc][2])
                cp("dve", utok_t[:, tb + 1, :], bk[:], [bb], [b_utok[tb + 1]])
                yield 8 * MMUS
            pws = ring_next(U_POOLW, 0)

            def pool_lin(gq):
                bk2, bb2 = next_bank()
                mm(bk2[:], pws[0][:, gq * 128:(gq + 1) * 128], og_t[:, gq, :], True, True, [pws[1], b_og[gq]], [bb2])
                ts("dve", mixT_t[:, gq, :], bk2[:], gains_t[:, 4, gq:gq + 1], None, ALU.mult, None, [bb2, b_gains],
                   [b_mixT[gq]])

            for gq in range(4):
                bk, bb = next_bank()
                for tb in range(4):
                    o_ = bk[:, tb * 128:(tb + 1) * 128]
                    cur = utok_t[:, tb + 1, gq * 128:(gq + 1) * 128]
                    prv = utok_t[:, tb, gq * 128:(gq + 1) * 128]
                    if first and tb == 0:
                        mm(o_, cur, pband_t[:, 8 + gq, :], True, True, [b_utok[1], b_pband], [bb])
                    else:
                        mm(o_, cur, pband_t[:, gq, :], True, False, [b_utok[tb + 1], b_pband], [bb])
                        mm(o_, prv, pband_t[:, 4 + gq, :], False, True, [b_utok[tb], b_pband], [bb])
                cp("act", og_t[:, gq, :], bk[:], [bb], [b_og[gq]])
                if gq > 0:
                    pool_lin(gq - 1)
                yield 2.5
            pool_lin(3)
            ring_done(pws[2])
            cp("pool", utok_t[:, 0, :], utok_t[:, 4, :], [b_utok[4]], [b_utok[0]])
            for tb in range(4):
                bk, bb = next_bank()
                bkb = bk[:].bitcast(BF16)
                for hh in range(4):
                    tr(bkb[:, hh * 128:(hh + 1) * 128], krel_t[:, hh, tb * 128:(tb + 1) * 128], [b_krel[hh]], [bb])
                cp("act", ktok_t[:, tb, :], bkb[:, 0:512], [bb], [b_ktok[tb]])
                bk, bb = next_bank()
                for hh in range(4):
                    mm(bk[:, hh * 128:(hh + 1) * 128], krel_t[:, hh, tb * 128:(tb + 1) * 128],
                       qG_t[:, hh, tb * 128:(tb + 1) * 128], True, True, [b_krel[hh], b_qG[hh]], [bb])
                tt("dve", sTm_t[:, tb, :].rearrange("p (h t) -> p h t", h=4),
                   bk[:].rearrange("p (h t) -> p h t", h=4),
                   mask_t[:].unsqueeze(1).to_broadcast([128, 4, 128]), ALU.mult, [bb, b_mask], [b_sTm[tb]])
                yield 2.0
                for half in range(2):
                    c = 2 * tb + half
                    pi = c % 2
                    tt("pool", Sbf_t[:, c, :, :], P_t[:, pi, :, :],
                       dec_t[:, :, c:c + 1].to_broadcast([128, 4, 128]), ALU.mult,
                       [b_P[pi][0], b_P[pi][1], b_P[pi][2], b_P[pi][3], b_dec, b_dec0], [b_Sbf[c]])
                    bk, bb = next_bank()
                    ps = slice(half * 64, half * 64 + 64)
                    for hh in range(4):
                        mm(bk[:, hh * 128:(hh + 1) * 128], ktok_t[ps, tb, hh * 128:(hh + 1) * 128],
                           vtok_t[ps, tb, hh * 128:(hh + 1) * 128], True, True, [b_ktok[tb], b_vtok[tb]], [bb])
                    for hh in range(4):
                        stt(P_t[:, 1 - pi, hh, :], P_t[:, pi, hh, :], dec_t[:, hh, c:c + 1],
                            bk[:, hh * 128:(hh + 1) * 128], ALU.mult, ALU.add,
                            [b_P[pi][hh], b_dec, b_dec0, bb], [b_P[1 - pi][hh]])
                    yield 2.0
            cp("dve", dec_t[:, :, 0:1], dec_t[:, :, 8:9], [b_dec], [b_dec0])
            def hgrn_fin(hh):
                o2 = hh % 2
                bk2, bb2 = next_bank()
                mm(bk2[:], ones_t[:], osq_t[:, o2, :], True, True, [b_ones, b_osq[o2]], [bb2])
                act(lnb_t[:, o2, :], bk2[:], AF.Ln, [bb2], [b_lnb[o2]], scale=1.0 / 128.0, bias=EPS)
                act(lnb_t[:, o2, :], lnb_t[:, o2, :], AF.Exp, [b_lnb[o2]], [b_lnb[o2]], scale=-0.5)
                stt(mixT_t[:, 4 + hh, :], og_t[:, hh, :], gains_t[:, 4, 4 + hh:5 + hh], lnb_t[:, o2, :], ALU.mult, ALU.mult,
                    [b_og[hh], b_lnb[o2], b_gains], [b_mixT[4 + hh]])

            for hh in range(4):
                bk, bb = next_bank()
                for tb in range(4):
                    mm(bk[:, tb * 128:(tb + 1) * 128], vtok_t[:, tb, hh * 128:(hh + 1) * 128],
                       sTm_t[:, tb, hh * 128:(hh + 1) * 128], True, False, [b_vtok[tb], b_sTm[tb]], [bb])
                    for half in range(2):
                        c = 2 * tb + half
                        cs = slice(tb * 128 + half * 64, tb * 128 + half * 64 + 64)
                        mm(bk[:, cs], Sbf_t[:, c, hh, :], qG_t[:, hh, cs], False, half == 1,
                           [b_Sbf[c], b_qG[hh]], [bb])
                o2 = hh % 2
                act(osq_t[:, o2, :], bk[:], AF.Square, [bb], [b_osq[o2]])
                tt("dve", og_t[:, hh, :], bk[:], gsil_t[:, hh, :], ALU.mult, [bb, b_gsil[hh]], [b_og[hh]])
                if hh > 0:
                    hgrn_fin(hh - 1)
                yield 5.0
            hgrn_fin(3)
            yield 2.0
            junkY = osq_t[:, :, :].rearrange("p a b -> p (a b)")
            yield from proj_tokmajor(hb, mixT_t, b_mixT, U_WOUT,
                                     after_tb=lambda tb: norm_square(hb, 4, tb, junkY, b_osq))
            yield from rms_norm_T(hb, 4, xnTY_t, b_xnTY, 1, 1, junkY, b_osq, squares_done=True)

        memn_half = arB_t[:, 0:1024]
        memnT_v = arB_t[:, 1024:2048].bitcast(BF16).rearrange("p (k m) -> p k m", k=8)
        b_memf = b_expT[0] + b_expT[1]

        def kv_gen(s_i, first_block_loaded=False):
            junk = rl_t[:, :, :].rearrange("p a b -> p (a b)")
            for mb in range(2):
                if not (mb == 0 and first_block_loaded):
                    dma(memn_half, mem_d[s_i * NMEM + mb * 128:s_i * NMEM + (mb + 1) * 128, :], [], b_memf)
                act(junk, memn_half, AF.Square, b_memf, b_rl + [b_ss[1]], accum_out=ss_t[:, 4 + mb:5 + mb])
                act(lnv_t[:, 4 + mb:5 + mb], ss_t[:, 4 + mb:5 + mb], AF.Ln, [b_ss[1]], [b_lnv[1]], scale=1.0 / D, bias=EPS)
                act(rstd_t[:, 4 + mb:5 + mb], lnv_t[:, 4 + mb:5 + mb], AF.Exp, [b_lnv[1]], [b_rstd[1]], scale=-0.5)
                act(xn_t[:, mb, :], memn_half, AF.Copy, b_memf + [b_rstd[1]], [b_xn[mb]], scale=rstd_t[:, 4 + mb:5 + mb])
                yield 6.0
                bk, bb = next_bank()
                bkb = bk[:].bitcast(BF16)
                for kc in range(8):
                    tr(bkb[:, kc * 128:(kc + 1) * 128], xn_t[:, mb, kc * 128:(kc + 1) * 128], [b_xn[mb]], [bb])
                tt("dve", memnT_v[:, :, mb * 128:(mb + 1) * 128], bkb.rearrange("p (k m) -> p k m", k=8),
                   gains_t[:, 2, :].unsqueeze(2).to_broadcast([128, 8, 128]), ALU.mult, [bb, b_gains], b_rden)
                yield 2.0
            for c in range(8):
                bk, bb = feat_group(U_XK + c, memnT_v, b_rden, n=256)
                cp("act" if c % 2 == 0 else "dve", kT_t[:, c, :], bk[:, 0:256], [bb], [b_kT])
                yield 8 * 0.11
            for half in range(2):
                us = [ring_next(U_XV + kc, half) for kc in range(8)]
                for mb in range(2):
                    bk, bb = next_bank()
                    for kc in range(8):
                        mm(bk[:], memnT_v[:, kc, mb * 128:(mb + 1) * 128], us[kc][0], kc == 0, kc == 7,
                           b_rden + [us[kc][1]], [bb])
                        if mb == 1:
                            ring_done(us[kc][2])
                    cp("act" if half == 0 else "dve", vmem_t[:, mb, half * 512:(half + 1) * 512], bk[:],
                       [bb], [b_vmem])
                    yield 8 * MMUS

        def phaseX(g, s_i, t_i):
            hb = g % 2
            for c in range(8):
                bk, bb = feat_group(U_XQ + c, xnTY_t, b_xnTY)
                if c % 2 == 0:
                    act(qT_t[:, c, :], bk[:], AF.Copy, [bb], [b_qT[c]], scale=1.0 / 16.0)
                else:
                    ts("dve", qT_t[:, c, :], bk[:], 1.0 / 16.0, None, ALU.mult, None, [bb], [b_qT[c]])
                yield 8 * MMUS
            yield -1.0
            def attn_scores(a):
                e2 = a % 2
                for mc in range(2):
                    bk, bb = next_bank()
                    for j in range(2):
                        mm(bk[:], kT_t[:, 2 * a + j, mc * 128:(mc + 1) * 128], qT_t[:, 2 * a + j, :], j == 0, j == 1,
                           [b_kT, b_qT[2 * a + j]], [bb])
                    act(expT_all[:, e2, mc, :], bk[:], AF.Exp, [bb], [b_expT[e2][mc]])

            def attn_pv(a):
                e2 = a % 2
                bk, bb = next_bank()
                for mc in range(2):
                    mm(bk[:], ones_t[:], expT_all[:, e2, mc, :], mc == 0, mc == 1, [b_ones, b_expT[e2][mc]], [bb])
                act(rdenv(e2), bk[:], AF.Ln, [bb], [b_rden[e2]])
                act(rdenv(e2), rdenv(e2), AF.Exp, [b_rden[e2]], [b_rden[e2]], scale=-1.0)
                for j in range(2):
                    bk, bb = next_bank()
                    for mc in range(2):
                        mm(bk[:], vmem_t[:, mc, (2 * a + j) * 128:(2 * a + j + 1) * 128], expT_all[:, e2, mc, :],
                           mc == 0, mc == 1, [b_vmem, b_expT[e2][mc]], [bb])
                    tt("dve", attnT_t[:, 2 * a + j, :], bk[:], rdenv(e2), ALU.mult, [bb, b_rden[e2]],
                       [b_attnT[2 * a + j]])

            attn_scores(0)
            yield 3.0
            for a in range(1, 4):
                attn_scores(a)
                yield 3.0
                attn_pv(a - 1)
                yield 5.0
            attn_pv(3)
            yield 4.0
            junkX = rl_t[:, :, :].rearrange("p a b -> p (a b)")
            yield from proj_tokmajor(hb, attnT_t, b_attnT, U_XO,
                                     after_tb=lambda tb: norm_square(hb, 0, tb, junkX, b_rl))
            if t_i == ntiles - 1 and s_i + 1 < nseq:
                yield from kv_gen(s_i + 1)
            yield from rms_norm_T(hb, 0, xnTX_t, b_xnTX, 3, 0, junkX, b_rl, y1=NORM_Y1C, y3=4.0, squares_done=True)
            for gq in range(4):
                gb = gq % 2
                for jj in range(8):
                    bk, bb = feat_group(U_WUP + 8 * gq + jj, xnTX_t, b_xnTX)
                    r2 = jj % 2
                    act(rl_t[:, r2, :], bk[:], AF.Relu, [bb], [b_rl[r2]])
                    tt("pool", aT_all[:, gb, jj, :], rl_t[:, r2, :], rl_t[:, r2, :], ALU.mult, [b_rl[r2]],
                       [b_arX[8 * gb + jj]])
                    yield 8 * MMUS
                for half in range(2):
                    us = [ring_next(U_WDN + 8 * gq + jj, half) for jj in range(8)]
                    for tb in range(4):
                        bk, bb = next_bank()
                        for jj in range(8):
                            mm(bk[:], aT_all[:, gb, jj, tb * 128:(tb + 1) * 128], us[jj][0], jj == 0, jj == 7,
                               [b_arX[8 * gb + jj], us[jj][1]], [bb])
                            if tb == 3:
                                ring_done(us[jj][2])
                        hs = hv(hb, tb)[:, half * 512:(half + 1) * 512]
                        tt("dve", hs, bk[:], hs, ALU.add, [bb, b_h[hb][tb]], [b_h[hb][tb]])
                        if gq == 3 and half == 1:
                            norm_square(hb, 3, tb, junkX, b_rl)
                        yield 8 * MMUS
            act(lnv_t[:, 12:16], ss_t[:, 12:16], AF.Ln, [b_ss[3]], [b_lnv[3]], scale=1.0 / D, bias=EPS)
            act(rstd_t[:, 12:16], lnv_t[:, 12:16], AF.Exp, [b_lnv[3]], [b_rstd[3]], scale=-0.5)
            for tb in range(4):
                stt(hv(hb, tb), hv(hb, tb), rstd_t[:, 12 + tb:13 + tb], gfin_t[:], ALU.mult, ALU.mult,
                    [b_h[hb][tb], b_rstd[3], b_gfin], [b_h[hb][tb]])
                if tb % 2 == 1:
                    hf = tb // 2
                    st = dma(out_d[g * T + hf * 256:g * T + (hf + 1) * 256, :].rearrange("(tb p) d -> p tb d", p=128),
                             hv_all(hb)[:, 2 * hf:2 * hf + 2, :], b_h[hb][2 * hf:2 * hf + 2], [])
                    out_stores.append(st)
            yield 8.0

        out_stores = []
        ntot = nseq * ntiles

        def load_x(g):
            buf = g % 2
            for hf in range(2):
                dma(hv_all(buf)[:, 2 * hf:2 * hf + 2, :],
                    x_d[g * T + hf * 256:g * T + (hf + 1) * 256, :].rearrange("(tb p) d -> p tb d", p=128),
                    [], b_h[buf][2 * hf:2 * hf + 2])

        def drive(gx, gy, head_start=0.0, gate=True, x_delay=0.0):
            tx = x_delay
            ty = None if (gx is not None and gate) else 0.0
            while gx is not None or gy is not None:
                if gx is not None and (gy is None or ty is None or tx <= ty):
                    try:
                        c = next(gx)
                        if c < 0:
                            if ty is None:
                                ty = tx + head_start
                        else:
                            tx += c
                    except StopIteration:
                        gx = None
                        if ty is None:
                            ty = tx
                else:
                    try:
                        ty += next(gy)
                    except StopIteration:
                        gy = None

        def coords(g):
            return g, g // ntiles, g % ntiles

        load_x(0)
        if not record:
            ring_fill()
        if ntot > 1:
            load_x(1)
        drive(kv_gen(0), phaseA(*coords(0)), gate=False, x_delay=70.0)
        for g in range(ntot):
            gy = phaseA(*coords(g + 1)) if g + 1 < ntot else None
            if interleave:
                drive(phaseX(*coords(g)), gy, head_start=x_head)
            else:
                drive(phaseX(*coords(g)), None)
                drive(None, gy)
            if g + 2 < ntot:
                load_x(g + 2)
        if not record:
            conv_upto(NU - 1)

        if record:
            return rec_units
        _DEBUG_INFO["sb_addr"] = dict(sb_addr)
        S.emit(block, sems, dsems, final_wait_ops=out_stores)
    return nc


_CACHE = {}


def kernel(**inputs):
    x = np.asarray(inputs["x"], np.float32)
    mem = np.asarray(inputs["mem"], np.float32)
    B = x.shape[0]
    seq = x.shape[1]
    ntiles = seq // T
    nseq = B // NCORES
    wall, gains, theta, gfin = _host_layout(inputs)
    ident, cmask, pband = _host_consts()
    key = (ntiles, nseq)
    if key not in _CACHE:
        _CACHE[key] = build_program(ntiles=ntiles, nseq=nseq)
    nc = _CACHE[key]
    in_maps = []
    for c in range(NCORES):
        in_maps.append({
            "x": np.ascontiguousarray(x[c * nseq:(c + 1) * nseq].reshape(nseq * seq, D)),
            "mem": np.ascontiguousarray(mem[c * nseq:(c + 1) * nseq].reshape(nseq * NMEM, D)),
            "wall": wall, "gains": gains, "theta": theta, "gfin": gfin,
            "ident": ident, "cmask": cmask, "pband": pband,
        })
    res = run_bass_kernel_spmd(nc, in_maps, core_ids=list(range(NCORES)))
    outs = [np.asarray(r["out"]).reshape(nseq, seq, D) for r in res.results]
    return np.concatenate(outs, axis=0).astype(np.float32)
```
